# Optimizing a Trainium2 kernel written in Bass

```python
import jax, jax.numpy as jnp
from jax import lax
import numpy as np

D_MODEL = 1024
BATCH = 8
SEQ = 4096
DEPTH = 1

CHUNK = 64
EPS = 1e-6
SSD_EXPAND = 2
SSD_INNER = SSD_EXPAND * D_MODEL
SSD_HEAD_DIM = 64
SSD_HEADS = SSD_INNER // SSD_HEAD_DIM
SSD_GROUPS = 4
SSD_HPG = SSD_HEADS // SSD_GROUPS
SSD_STATE = 128
SSD_CONV = 4
SSD_XBC = SSD_INNER + 2 * SSD_GROUPS * SSD_STATE
SCONV_DIM = D_MODEL
SCONV_K = 3
FFN_HIDDEN = ((8 * D_MODEL + 767) // 768) * 256
IN_SIZES = (SSD_INNER, SSD_XBC, SSD_HEADS, SCONV_DIM, SCONV_DIM, SCONV_DIM, D_MODEL, D_MODEL)
IN_DIM = sum(IN_SIZES)
IN_SPLITS = tuple(int(v) for v in np.cumsum(IN_SIZES)[:-1])

kernel_name = 'hybrid_ssd_shortconv_gated_block'


def rmsnorm(x, w):
    x32 = x.astype(jnp.float32)
    y = x32 * lax.rsqrt(jnp.mean(x32 * x32, axis=-1, keepdims=True) + EPS)
    return (y * w.astype(jnp.float32)).astype(x.dtype)


def causal_depthwise_conv(x, w):
    k = w.shape[0]
    return lax.conv_general_dilated(
        x, w[:, None, :].astype(x.dtype), window_strides=(1,), padding=[(k - 1, 0)],
        dimension_numbers=('NWC', 'WIO', 'NWC'), feature_group_count=x.shape[-1])


def ssd_chunked_scan(x, dt, a, bm, cm):
    bsz, seqlen = x.shape[0], x.shape[1]
    nc = seqlen // CHUNK

    def to_chunks(t):
        return jnp.moveaxis(t.reshape((bsz, nc, CHUNK) + t.shape[2:]), 1, 0)

    causal = jnp.tril(jnp.ones((CHUNK, CHUNK), dtype=bool))[None, :, :, None, None]

    def step(state, inp):
        xc, dtc, bc, cc = inp
        acum = jnp.cumsum(dtc * a, axis=1)
        seg = acum[:, :, None] - acum[:, None, :]
        decay = jnp.exp(jnp.where(causal, seg, -jnp.inf))
        cb = jnp.einsum('blgn,bsgn->blsg', cc, bc)
        xdt = xc * dtc[..., None]
        scores = cb[..., None] * decay
        y_diag = jnp.einsum('blsgr,bsgrp->blgrp', scores, xdt)
        y_off = jnp.einsum('blgn,bgrpn->blgrp', cc, state) * jnp.exp(acum)[..., None]
        to_end = jnp.exp(acum[:, -1:] - acum)
        new_state = (state * jnp.exp(acum[:, -1])[..., None, None]
                     + jnp.einsum('bsgn,bsgr,bsgrp->bgrpn', bc, to_end, xdt))
        return new_state, y_diag + y_off

    state0 = jnp.zeros((bsz, SSD_GROUPS, SSD_HPG, SSD_HEAD_DIM, SSD_STATE), jnp.float32)
    _, ys = lax.scan(step, state0, (to_chunks(x), to_chunks(dt), to_chunks(bm), to_chunks(cm)))
    return jnp.moveaxis(ys, 0, 1).reshape(x.shape)


def hybrid_mixer(h, w_in, ssd_conv_w, ssd_conv_b, dt_bias, a_log, d_skip, ssd_norm_w,
                 w_ssd_proj, sconv_w, w_sconv_proj, w_o):
    bsz, seqlen, _ = h.shape
    u = h @ w_in
    z, xbc, dt_raw, s_b, s_c, s_x, g_a, g_b = jnp.split(u, IN_SPLITS, axis=-1)

    xbc = jax.nn.silu(causal_depthwise_conv(xbc, ssd_conv_w) + ssd_conv_b)
    xs, bm, cm = jnp.split(xbc, (SSD_INNER, SSD_INNER + SSD_GROUPS * SSD_STATE), axis=-1)
    xs32 = xs.astype(jnp.float32).reshape(bsz, seqlen, SSD_GROUPS, SSD_HPG, SSD_HEAD_DIM)
    dt = jax.nn.softplus(dt_raw.astype(jnp.float32) + dt_bias.astype(jnp.float32))
    dt = dt.reshape(bsz, seqlen, SSD_GROUPS, SSD_HPG)
    a = -jnp.exp(a_log.astype(jnp.float32)).reshape(SSD_GROUPS, SSD_HPG)
    bm32 = bm.astype(jnp.float32).reshape(bsz, seqlen, SSD_GROUPS, SSD_STATE)
    cm32 = cm.astype(jnp.float32).reshape(bsz, seqlen, SSD_GROUPS, SSD_STATE)
    y = ssd_chunked_scan(xs32, dt, a, bm32, cm32)
    y = y + d_skip.astype(jnp.float32).reshape(SSD_GROUPS, SSD_HPG)[..., None] * xs32
    y = y.reshape(bsz, seqlen, SSD_INNER) * jax.nn.silu(z.astype(jnp.float32))
    yg = y.reshape(bsz, seqlen, SSD_GROUPS, SSD_INNER // SSD_GROUPS)
    yg = yg * lax.rsqrt(jnp.mean(yg * yg, axis=-1, keepdims=True) + EPS)
    y = (yg.reshape(bsz, seqlen, SSD_INNER) * ssd_norm_w.astype(jnp.float32)).astype(h.dtype)
    branch_a = y @ w_ssd_proj

    v = causal_depthwise_conv(s_c * s_x, sconv_w)
    branch_b = (s_b * v) @ w_sconv_proj

    merged = jax.nn.sigmoid(g_a) * branch_a + jax.nn.sigmoid(g_b) * branch_b
    return merged @ w_o


def swiglu(h, w_gate, w_up, w_down):
    return (jax.nn.silu(h @ w_gate) * (h @ w_up)) @ w_down


def setup_inputs(seed: int = 0) -> dict:
    key = jax.random.key(seed)
    ks = jax.random.split(key, 20)
    f32 = jnp.float32

    def nrm(k, shape, scale):
        return jax.random.normal(k, shape, f32) * scale

    x = jax.random.normal(ks[0], (BATCH, SEQ, D_MODEL), f32)
    dt_init = jnp.exp(jax.random.uniform(ks[5], (DEPTH, SSD_HEADS), f32, np.log(1e-3), np.log(1e-1)))
    dt_bias = dt_init + jnp.log(-jnp.expm1(-dt_init))
    return {
        'x': x,
        'norm_mix_w': 1.0 + nrm(ks[1], (DEPTH, D_MODEL), 0.05),
        'w_in': nrm(ks[2], (DEPTH, D_MODEL, IN_DIM), D_MODEL ** -0.5),
        'ssd_conv_w': nrm(ks[3], (DEPTH, SSD_CONV, SSD_XBC), SSD_CONV ** -0.5),
        'ssd_conv_b': nrm(ks[4], (DEPTH, SSD_XBC), 0.01),
        'dt_bias': dt_bias,
        'a_log': jnp.log(jax.random.uniform(ks[6], (DEPTH, SSD_HEADS), f32, 1.0, 16.0)),
        'd_skip': 1.0 + nrm(ks[7], (DEPTH, SSD_HEADS), 0.1),
        'ssd_norm_w': 1.0 + nrm(ks[8], (DEPTH, SSD_INNER), 0.05),
        'w_ssd_proj': nrm(ks[9], (DEPTH, SSD_INNER, D_MODEL), SSD_INNER ** -0.5),
        'sconv_w': nrm(ks[10], (DEPTH, SCONV_K, SCONV_DIM), SCONV_K ** -0.5),
        'w_sconv_proj': nrm(ks[11], (DEPTH, SCONV_DIM, D_MODEL), SCONV_DIM ** -0.5),
        'w_o': nrm(ks[12], (DEPTH, D_MODEL, D_MODEL), D_MODEL ** -0.5),
        'norm_ffn_w': 1.0 + nrm(ks[13], (DEPTH, D_MODEL), 0.05),
        'w_gate': nrm(ks[14], (DEPTH, D_MODEL, FFN_HIDDEN), D_MODEL ** -0.5),
        'w_up': nrm(ks[15], (DEPTH, D_MODEL, FFN_HIDDEN), D_MODEL ** -0.5),
        'w_down': nrm(ks[16], (DEPTH, FFN_HIDDEN, D_MODEL), FFN_HIDDEN ** -0.5),
        'final_norm_w': 1.0 + nrm(ks[17], (D_MODEL,), 0.05),
    }


def reference(x, norm_mix_w, w_in, ssd_conv_w, ssd_conv_b, dt_bias, a_log, d_skip, ssd_norm_w,
              w_ssd_proj, sconv_w, w_sconv_proj, w_o, norm_ffn_w, w_gate, w_up, w_down,
              final_norm_w):
    h = x
    for i in range(DEPTH):
        h = h + hybrid_mixer(rmsnorm(h, norm_mix_w[i]), w_in[i], ssd_conv_w[i], ssd_conv_b[i],
                             dt_bias[i], a_log[i], d_skip[i], ssd_norm_w[i], w_ssd_proj[i],
                             sconv_w[i], w_sconv_proj[i], w_o[i])
        h = h + swiglu(rmsnorm(h, norm_ffn_w[i]), w_gate[i], w_up[i], w_down[i])
    return rmsnorm(h, final_norm_w)
```

```python
import numpy as np
from contextlib import ExitStack
import concourse.bass as bass
import concourse.mybir as mybir
from concourse.bass_utils import run_bass_kernel_spmd

F32 = mybir.dt.float32
BF16 = mybir.dt.bfloat16
ALU = mybir.AluOpType
AF = mybir.ActivationFunctionType

D = 1024
TT = 512
NST = 4
DIN = 10272
FF = 2816
EPS = 1e-6
WSLOT = 4096
NWSLOT = 3


class Res:
    __slots__ = ("name", "writes", "reads", "overlaps")

    def __init__(self, name):
        self.name = name
        self.writes = {}
        self.reads = {}
        self.overlaps = []


class Sched:
    def __init__(self, nc, stack, same_engine_sync=False):
        self.nc = nc
        self.stack = stack
        self.engs = {"pe": nc.tensor, "act": nc.scalar, "dve": nc.vector,
                     "pool": nc.gpsimd, "sp": nc.sync}
        self.sem, self.count, self.known, self.snap = {}, {}, {}, {}
        self.same = same_engine_sync
        for k in self.engs:
            self.sem[k] = stack.enter_context(nc.semaphore("prog_" + k))
            self.count[k] = 0
            self.known[k] = {}
            self.snap[k] = [None]
        self.dma_sems = {}
        self.nwaits = 0
        self.ninstr = 0

    def dma_sem(self, name):
        if name not in self.dma_sems:
            s = self.stack.enter_context(self.nc.semaphore("dma_" + name))
            self.dma_sems[name] = s
            self.sem[name] = s
            self.count[name] = 0
            self.snap[name] = {}
        return name

    SAME_SYNC = ("act", "dve", "pool")

    def _need(self, eng, deps):
        raw, other = deps
        kn = self.known[eng]
        allk = dict(other)
        for k, v in raw.items():
            if allk.get(k, 0) < v:
                allk[k] = v
        for key, val in allk.items():
            if key == eng:
                if not (eng in self.SAME_SYNC and raw.get(key, 0) > kn.get(key, 0)):
                    continue
                val = raw[key]
            if kn.get(key, 0) >= val:
                continue
            self.engs[eng].wait_ge(self.sem[key], val)
            self.nwaits += 1
            kn[key] = val
            sn = self.snap[key]
            s = sn[val] if isinstance(sn, list) else sn.get(val)
            if s:
                for k2, v2 in s.items():
                    if kn.get(k2, 0) < v2:
                        kn[k2] = v2

    @staticmethod
    def _deps(reads, writes):
        raw, other = {}, {}

        def add(t, d):
            for k, v in d.items():
                if t.get(k, 0) < v:
                    t[k] = v
        for r in reads:
            add(raw, r.writes)
        for w in writes:
            add(other, w.writes)
            add(other, w.reads)
            for o in w.overlaps:
                add(other, o.writes)
                add(other, o.reads)
        return raw, other

    def op(self, eng, fn, reads=(), writes=()):
        self._need(eng, self._deps(reads, writes))
        ins = fn(self.engs[eng])
        self.count[eng] += 1
        c = self.count[eng]
        ins.then_inc(self.sem[eng], 1)
        self.ninstr += 1
        self.snap[eng].append(dict(self.known[eng]))
        for r in reads:
            if r.reads.get(eng, 0) < c:
                r.reads[eng] = c
        for w in writes:
            w.writes = {eng: c}
            w.reads = {}

    def dma(self, queue, semname, out, in_, reads=(), writes=(), **kw):
        self.dma_sem(semname)
        self._need(queue, self._deps(reads, writes))
        ins = self.engs[queue].dma_start(out=out, in_=in_, **kw)
        self.count[semname] += 16
        c = self.count[semname]
        ins.then_inc(self.sem[semname], 16)
        self.ninstr += 1
        self.snap[semname][c] = dict(self.known[queue])
        for r in reads:
            if r.reads.get(semname, 0) < c:
                r.reads[semname] = c
        for w in writes:
            w.writes = {semname: c}
            w.reads = {}

    def wait_all(self, eng, resources):
        deps = {}
        for r in resources:
            for d in (r.writes, r.reads):
                for k, v in d.items():
                    if deps.get(k, 0) < v:
                        deps[k] = v
        self._need(eng, ({}, deps))


def build(T, debug=False):
    NT = T // TT
    nc = bass.Bass("TRN2", target_bir_lowering=False)

    def din(name, shape):
        return nc.dram_tensor(name, list(shape), F32, kind="ExternalInput").ap()

    x_d = din("x", [T, D])
    nmw_d = din("norm_mix_w", [1, D])
    win_d = din("w_in", [1, D, DIN])
    cw_d = din("ssd_conv_w", [1, 4, 3072])
    cb_d = din("ssd_conv_b", [1, 3072])
    dtb_d = din("dt_bias", [1, 32])
    alog_d = din("a_log", [1, 32])
    dsk_d = din("d_skip", [1, 32])
    snw_d = din("ssd_norm_w", [1, 2048])
    wssd_d = din("w_ssd_proj", [1, 2048, D])
    scw_d = din("sconv_w", [1, 3, D])
    wsc_d = din("w_sconv_proj", [1, D, D])
    wo_d = din("w_o", [1, D, D])
    nfw_d = din("norm_ffn_w", [1, D])
    wg_d = din("w_gate", [1, D, FF])
    wu_d = din("w_up", [1, D, FF])
    wd_d = din("w_down", [1, FF, D])
    fnw_d = din("final_norm_w", [D])
    out_d = nc.dram_tensor("out", [T, D], F32, kind="ExternalOutput").ap()

    def dscr(name, shape):
        return nc.dram_tensor(name, list(shape), BF16, kind="Internal").ap()

    win_s = dscr("win_s", [D, DIN])
    wssd_s = dscr("wssd_s", [2048, D])
    wsc_s = dscr("wsc_s", [D, D])
    wo_s = dscr("wo_s", [D, D])
    wg_s = dscr("wg_s", [D, FF])
    wu_s = dscr("wu_s", [D, FF])
    wd_s = dscr("wd_s", [FF, D])

    with ExitStack() as st_:
        S = Sched(nc, st_)

        def sb(name, shape, dt):
            return st_.enter_context(nc.sbuf_tensor(name, list(shape), dt))

        def ps(name, shape, dt):
            return st_.enter_context(nc.psum_tensor(name, list(shape), dt))

        r_win, r_wssd, r_wsc, r_wo, r_wg, r_wu, r_wd = (Res(n) for n in
                                                        ("win", "wssd", "wsc", "wo", "wg", "wu", "wd"))
        wslots = [sb("wslot%d" % i, [128, WSLOT], BF16) for i in range(NWSLOT)]
        r_wslot = [Res("wslot%d" % i) for i in range(NWSLOT)]
        xres = sb("xres", [128, NST, D], F32)
        r_xres = [Res("xres%d" % i) for i in range(NST)]
        hnb = sb("hnb", [128, D], BF16)
        r_hnb = Res("hnb")
        hnT = sb("hnT", [128, 8, TT], BF16)
        r_hnT = [Res("hnT%d" % i) for i in range(8)]
        arena = sb("arena", [128, 24576], BF16)
        xs_all = arena[:, 0:8192].rearrange("p (s c) -> p s c", s=NST)
        BT = arena[:, 8192:10240].rearrange("p (g t) -> p g t", g=4)
        CT = arena[:, 10240:12288].rearrange("p (g t) -> p g t", g=4)
        uT = arena[:, 12288:16384].rearrange("p (j t) -> p j t", j=8)
        gsig = arena[:, 16384:24576].rearrange("p (j t) -> p j t", j=16)
        actT = arena[:, 0:11264].rearrange("p (j t) -> p j t", j=22)
        zs = [sb("zs%d" % i, [128, 2048], BF16) for i in range(2)]
        r_zs = [Res("zs%d" % i) for i in range(2)]
        r_xs = [Res("xs%d" % i) for i in range(NST)]
        r_BT = [Res("BT%d" % i) for i in range(4)]
        r_CT = [Res("CT%d" % i) for i in range(4)]
        r_uT = [Res("uT%d" % i) for i in range(8)]
        r_gsig = [Res("gsig%d" % i) for i in range(16)]
        r_actT = [Res("actT%d" % i) for i in range(22)]
        ph_ssd = r_xs + r_BT + r_CT
        ph_ffn = r_actT
        for r in ph_ssd:
            r.overlaps = ph_ffn
        for r in ph_ffn:
            r.overlaps = ph_ssd
        b_all = sb("b_all", [128, NST, 512], BF16)
        r_ball = [Res("ball%d" % i) for i in range(NST)]
        yT = sb("yT", [128, 16, TT], BF16)
        r_yT = [Res("yT%d" % i) for i in range(16)]
        state = sb("state", [128, 4, 512], F32)
        stateb = sb("stateb", [128, 4, 512], BF16)
        r_state = [Res("state%d" % i) for i in range(4)]
        r_stateb = [Res("stateb%d" % i) for i in range(4)]
        xdt = sb("xdt", [128, 2048], BF16)
        xdtw = sb("xdtw", [128, 2048], BF16)
        xsd = sb("xsd", [128, 2048], BF16)
        r_xdt, r_xdtw, r_xsd = Res("xdt"), Res("xdtw"), Res("xsd")
        cbm = sb("cbm", [128, 4, 128], BF16)
        r_cbm = Res("cbm")
        Rg = [sb("Rg%d" % i, [128, 8, 128], BF16) for i in range(2)]
        r_Rg = [Res("Rg%d" % i) for i in range(2)]
        Eg = [sb("Eg%d" % i, [128, 8, 128], BF16) for i in range(2)]
        r_Eg = [Res("Eg%d" % i) for i in range(2)]
        scr_ = [sb("scores%d" % i, [128, 8, 128], BF16) for i in range(2)]
        r_scr = [Res("scores%d" % i) for i in range(2)]
        yg4 = sb("yg4", [128, 2048], F32)
        r_yg4 = [Res("yg4_%d" % i) for i in range(4)]
        ygb = [sb("ygb0", [128, 512], BF16)] * 2
        r_ygb = [Res("ygb0")] * 2
        NTMP = 9
        tmpf = [sb("tmpf%d" % i, [128, 516], F32) for i in range(NTMP)]
        r_tmpf = [Res("tmpf%d" % i) for i in range(NTMP)]
        tmp_i = [0]

        def gettmp():
            i = tmp_i[0] % NTMP
            tmp_i[0] += 1
            return tmpf[i], r_tmpf[i]
        xsT = [sb("xsT%d" % i, [128, 512], BF16) for i in range(2)]
        r_xsT = [Res("xsT%d" % i) for i in range(2)]
        halo = sb("halo", [128, 24, 3], F32)
        r_halo = Res("halo")
        halo2 = sb("halo2", [128, 8, 2], F32)
        r_halo2 = Res("halo2")
        sm = sb("sm", [128, 1536], F32)
        r_sm = Res("sm")
        dhl = sb("dhl", [128, 2, 4, 32], BF16)
        r_dhl = Res("dhl")
        stat = sb("stat", [128, 32], F32)
        r_stat = Res("stat")
        fnw = sb("fnw", [128, D], F32)
        tri_le = sb("tri_le", [128, 128], BF16)
        mstrict = sb("mstrict", [128, 128], BF16)
        ones_b = sb("ones_b", [128, 128], BF16)
        ident_b = sb("ident_b", [128, 128], BF16)
        ident_f = tmpf[3][:, 0:128]
        mk_f = tmpf[2][:, 0:128]
        pin1 = tmpf[0][:, 0:128]
        pin2 = tmpf[1][:, 0:128]
        pt1 = sb("pt1", [128, 120], F32)
        pt2 = sb("pt2", [128, 56], F32)
        bc = sb("bc", [128, 4, 32], F32)
        wdt = sb("wdt", [128, 8, 32], BF16)
        r_const = Res("const")
        r_mk = Res("mk")

        banks = [ps("bank%d" % i, [128, 512], F32) for i in (0, 1, 2)]
        tpb = [ps("tp0", [128, 1024], BF16)]
        banks += [ps("bank%d" % i, [128, 512], F32) for i in (4, 5, 6)]
        tpb += [ps("tp1", [128, 1024], BF16)]
        r_bank = [Res("bank%d" % i) for i in range(6)]
        r_tpb = [Res("tp%d" % i) for i in range(2)]
        mm_i = [0]

        def getbank():
            i = mm_i[0] % 6
            mm_i[0] += 1
            return banks[i], r_bank[i]
        tp_i = [0]

        def gettp():
            i = tp_i[0] % 2
            tp_i[0] += 1
            return tpb[i], r_tpb[i]

        dbg_n = [0]

        def dbg(name, ap, reads, ti=0):
            if not debug or ti != 0:
                return
            dt_ = nc.dram_tensor("dbg_" + name, list(ap.shape), ap.dtype, kind="ExternalOutput").ap()
            S.dma("sp", "dbg", dt_, ap, reads=reads)
            dbg_n[0] += 1

        def cload(dst, src, **kw):
            S.dma("sp", "const", dst, src, writes=[r_const], **kw)
        S.op("pool", lambda e: e.memset(pin1[:], 0.0), writes=[r_const])
        S.op("pool", lambda e: e.memset(pin2[:], 0.0), writes=[r_const])
        cload(pin1[0:96, :], cw_d[0].rearrange("k (c p) -> (k c) p", p=128))
        cload(pin1[96:120, :], cb_d[0].rearrange("(c p) -> c p", p=128))
        cload(pin2[0:8, :], nmw_d[0].rearrange("(c p) -> c p", p=128))
        cload(pin2[8:16, :], nfw_d[0].rearrange("(c p) -> c p", p=128))
        cload(pin2[16:32, :], snw_d[0].rearrange("(c p) -> c p", p=128))
        cload(pin2[32:56, :], scw_d[0].rearrange("k (c p) -> (k c) p", p=128))
        cload(bc[:, 0, :], dtb_d.partition_broadcast(128))
        cload(bc[:, 1, :], alog_d.partition_broadcast(128))
        cload(bc[:, 2, :], dsk_d.partition_broadcast(128))
        cload(fnw[:], fnw_d.rearrange("(o d) -> o d", o=1).partition_broadcast(128))

        def mkmask(dst_bf, cmp_op, base):
            S.op("pool", lambda e: e.memset(mk_f[:], 1.0), writes=[r_mk])
            S.op("pool", lambda e: e.affine_select(out=mk_f[:], in_=mk_f[:], pattern=[[1, 128]],
                                                   compare_op=cmp_op, fill=0.0, base=base,
                                                   channel_multiplier=-1), reads=[r_mk], writes=[r_mk])
            S.op("pool", lambda e: e.tensor_copy(out=dst_bf[:], in_=mk_f[:]), reads=[r_mk], writes=[r_const])
        mkmask(tri_le, ALU.is_ge, 0)
        S.op("pool", lambda e: e.memset(mk_f[:], 1.0), writes=[r_mk])
        S.op("pool", lambda e: e.affine_select(out=mk_f[:], in_=mk_f[:], pattern=[[-1, 128]],
                                               compare_op=ALU.is_ge, fill=0.0, base=-1,
                                               channel_multiplier=1), reads=[r_mk], writes=[r_mk])
        S.op("pool", lambda e: e.tensor_copy(out=mstrict[:], in_=mk_f[:]), reads=[r_mk], writes=[r_const])
        S.op("pool", lambda e: e.memset(ones_b[:], 1.0), writes=[r_const])
        S.op("pool", lambda e: e.memset(ident_f[:], 1.0), writes=[r_const])
        S.op("pool", lambda e: e.affine_select(out=ident_f[:], in_=ident_f[:], pattern=[[1, 128]],
                                               compare_op=ALU.is_equal, fill=0.0, base=0,
                                               channel_multiplier=-1), reads=[r_const], writes=[r_const])
        S.op("pool", lambda e: e.tensor_copy(out=ident_b[:], in_=ident_f[:]), reads=[r_const], writes=[r_const])
        S.op("pool", lambda e: e.memset(halo[:], 0.0), writes=[r_halo])
        S.op("pool", lambda e: e.memset(halo2[:], 0.0), writes=[r_halo2])
        for g in range(4):
            S.op("pool", lambda e, g=g: e.memset(state[:, g, :], 0.0), writes=[r_state[g]])
            S.op("pool", lambda e, g=g: e.memset(stateb[:, g, :], 0.0), writes=[r_stateb[g]])
        bk, rb_ = getbank()
        S.op("pe", lambda e: e.transpose(out=bk[:, 0:120], in_=pin1[0:120, :], identity=ident_f[0:120, 0:120]),
             reads=[r_const], writes=[rb_])
        S.op("dve", lambda e: e.tensor_copy(out=pt1[:], in_=bk[:, 0:120]), reads=[rb_], writes=[r_const])
        bk, rb_ = getbank()
        S.op("pe", lambda e: e.transpose(out=bk[:, 0:56], in_=pin2[0:56, :], identity=ident_f[0:56, 0:56]),
             reads=[r_const], writes=[rb_])
        S.op("dve", lambda e: e.tensor_copy(out=pt2[:], in_=bk[:, 0:56]), reads=[rb_], writes=[r_const])
        S.op("act", lambda e: e.activation(out=bc[:, 3, :], in_=bc[:, 1, :], func=AF.Exp), reads=[r_const], writes=[r_const])
        S.op("dve", lambda e: e.tensor_scalar(out=bc[:, 1, :], in0=bc[:, 3, :], scalar1=-1.0, scalar2=None, op0=ALU.mult),
             reads=[r_const], writes=[r_const])
        w_in2 = win_d[0]
        ci = [0]

        def cast(dst, src, res):
            S.dma("pool", "cast%d" % ci[0], dst, src, writes=[res])
            ci[0] += 1
        r_win_parts = [Res("win_p%d" % i) for i in range(8)]
        cast(win_s[:, 10240:10272], w_in2[:, 5120:5152], r_win_parts[2])
        cast(win_s[:, 2048:5120], w_in2[:, 2048:5120], r_win_parts[1])
        cast(win_s[:, 0:2048], w_in2[:, 0:2048], r_win_parts[0])
        for t3 in range(3):
            dst = win_s[:, 5120:8192].rearrange("k (j t c) -> k j t c", j=8, t=3, c=128)[:, :, t3, :]
            src = w_in2[:, 5152 + t3 * 1024: 5152 + (t3 + 1) * 1024].rearrange("k (j c) -> k j c", c=128)
            cast(dst, src, r_win_parts[3 + t3])
        cast(win_s[:, 8192:10240], w_in2[:, 8224:10272], r_win_parts[6])
        r_win_all = r_win_parts[:7]
        late_casts = []

        def cast_chunks(dst, src, reslist, name, nch):
            rows = dst.shape[0]
            step = rows // nch
            for i in range(nch):
                r_ = Res("lc_%s%d" % (name, i))
                reslist.append(r_)
                late_casts.append(lambda i=i, r_=r_: S.dma("pool", "lc_" + name, dst[i * step:(i + 1) * step, :],
                                                           src[i * step:(i + 1) * step, :], writes=[r_]))
        r_wssd, r_wsc, r_wo, r_wg, r_wu, r_wd = [], [], [], [], [], []
        cast_chunks(wssd_s, wssd_d[0], r_wssd, "wssd", 4)
        cast_chunks(wsc_s, wsc_d[0], r_wsc, "wsc", 2)
        cast_chunks(wo_s, wo_d[0], r_wo, "wo", 2)
        cast_chunks(wg_s, wg_d[0], r_wg, "wg", 4)
        cast_chunks(wu_s, wu_d[0], r_wu, "wu", 4)
        cast_chunks(wd_s, wd_d[0], r_wd, "wd", 4)

        S.dma("sp", "const", wdt[:], win_s.rearrange("(kc p) n -> p kc n", p=128)[:, :, 10240:10272],
              reads=[r_win_parts[2]], writes=[r_const])
        RC = [r_const]
        for i in range(4):
            for src in (r_const, r_mk):
                for d_ in (src.reads, src.writes):
                    for k_, v_ in d_.items():
                        if r_tmpf[i].reads.get(k_, 0) < v_:
                            r_tmpf[i].reads[k_] = v_

        wsl_i = [0]

        def wload(src_view, kcn, ncols, src_res):
            i = wsl_i[0] % NWSLOT
            wsl_i[0] += 1
            v = wslots[i][:, 0:kcn * ncols].rearrange("p (k n) -> p k n", k=kcn)
            S.dma("sp", "w%d" % i, v, src_view, reads=src_res, writes=[r_wslot[i]])
            return v, r_wslot[i]

        win_v = win_s.rearrange("(kc p) n -> p kc n", p=128)

        def rmsnorm_to_T(sti, wcol0, tag):
            xs_ = xres[:, sti, :]
            S.op("act", lambda e: e.activation(out=hnb[:], in_=xs_, func=AF.Square, accum_out=stat[:, 0:1]),
                 reads=[r_xres[sti]], writes=[r_hnb, r_stat])
            S.op("act", lambda e: e.activation(out=stat[:, 1:2], in_=stat[:, 0:1], func=AF.Ln, scale=1.0 / D, bias=EPS),
                 reads=[r_stat], writes=[r_stat])
            S.op("act", lambda e: e.activation(out=stat[:, 2:3], in_=stat[:, 1:2], func=AF.Exp, scale=-0.5),
                 reads=[r_stat], writes=[r_stat])
            S.op("act", lambda e: e.activation(out=hnb[:], in_=xs_, func=AF.Copy, scale=stat[:, 2:3]),
                 reads=[r_xres[sti], r_stat], writes=[r_hnb])
            tp, rtp = gettp()
            for kc in range(8):
                S.op("pe", lambda e, kc=kc: e.transpose(out=tp[:, kc * 128:(kc + 1) * 128],
                                                        in_=hnb[:, kc * 128:(kc + 1) * 128], identity=ident_b[:]),
                     reads=[r_hnb] + RC, writes=[rtp])
            S.op("dve", lambda e: e.tensor_tensor(
                out=hnT[:, :, sti * 128:(sti + 1) * 128],
                in0=tp[:].rearrange("p (k t) -> p k t", k=8),
                in1=pt2[:, wcol0:wcol0 + 8].unsqueeze(2).to_broadcast([128, 8, 128]), op=ALU.mult),
                reads=[rtp] + RC, writes=r_hnT)

        def proj_fm(wv, rw, c0, kcn, rhs_of, rhs_res):
            bk, rbk = getbank()
            for kc in range(kcn):
                S.op("pe", lambda e, kc=kc: e.matmul(bk[:], lhsT=wv[:, kc, c0:c0 + 128], rhs=rhs_of(kc),
                                                     start=(kc == 0), stop=(kc == kcn - 1)),
                     reads=[rw, rhs_res[kc]], writes=[rbk])
            return bk, rbk

        for ti in range(NT):
            t0 = ti * TT
            for sti in range(NST):
                S.dma("sp", "x%d" % sti, xres[:, sti, :], x_d[t0 + sti * 128:t0 + (sti + 1) * 128, :],
                      writes=[r_xres[sti]])
            for sti in range(NST):
                rmsnorm_to_T(sti, 0, "mix")

            dbg('hnT', hnT[:], r_hnT, ti)
            p4w = {}

            def p4_s1(c):
                blk, j4 = divmod(c, 4)
                if j4 == 0:
                    p4w[blk] = wload(win_v[:, :, 2048 + blk * 512:2048 + (blk + 1) * 512], 8, 512, [r_win_parts[1]])
                wv, rw = p4w[blk]
                bk, rbk = proj_fm(wv, rw, j4 * 128, 8, lambda kc: hnT[:, kc, :], r_hnT)
                xb, rxb = gettmp()
                acc, racc = gettmp()
                S.op("act", lambda e: e.activation(out=xb[:, 3:515], in_=bk[:], func=AF.Copy),
                     reads=[rbk], writes=[rxb])
                S.op("act", lambda e: e.activation(out=xb[:, 0:3], in_=halo[:, c, :], func=AF.Copy),
                     reads=[r_halo], writes=[rxb])
                S.op("act", lambda e: e.activation(out=halo[:, c, :], in_=bk[:, 509:512], func=AF.Copy),
                     reads=[rbk], writes=[r_halo])
                S.op("act", lambda e: e.activation(out=acc[:, 0:512], in_=bk[:], func=AF.Identity,
                                                   scale=pt1[:, 72 + c:73 + c], bias=pt1[:, 96 + c:97 + c]),
                     reads=[rbk] + RC, writes=[racc])
                return (c, xb, rxb, acc, racc)

            def p4_s2(items):
                for k in range(3):
                    for (c, xb, rxb, acc, racc) in items:
                        S.op("dve", lambda e: e.scalar_tensor_tensor(
                            out=acc[:, 0:512], in0=xb[:, k:k + 512], scalar=pt1[:, k * 24 + c:k * 24 + c + 1],
                            in1=acc[:, 0:512], op0=ALU.mult, op1=ALU.add),
                            reads=[rxb, racc] + RC, writes=[racc])
                for (c, xb, rxb, acc, racc) in items:
                    if c < 20:
                        xo, rxo = xsT[c % 2], r_xsT[c % 2]
                        if c >= 16:
                            S.op("act", lambda e: e.activation(out=BT[:, c - 16, :], in_=acc[:, 0:512], func=AF.Silu),
                                 reads=[racc], writes=[r_BT[c - 16]])
                            src, rsrc = BT[:, c - 16, :], r_BT[c - 16]
                        else:
                            S.op("act", lambda e: e.activation(out=xo[:], in_=acc[:, 0:512], func=AF.Silu),
                                 reads=[racc], writes=[rxo])
                            src, rsrc = xo[:], rxo
                        tp, rtp = gettp()
                        for sti in range(NST):
                            S.op("pe", lambda e, sti=sti: e.transpose(out=tp[:, sti * 128:(sti + 1) * 128],
                                                                      in_=src[:, sti * 128:(sti + 1) * 128],
                                                                      identity=ident_b[:]),
                                 reads=[rsrc] + RC, writes=[rtp])
                        if c < 16:
                            S.op("dve", lambda e: e.tensor_copy(out=xs_all[:, :, c * 128:(c + 1) * 128],
                                                                in_=tp[:, 0:512].rearrange("p (s t) -> p s t", s=NST)),
                                 reads=[rtp], writes=r_xs)
                        else:
                            S.op("dve", lambda e: e.tensor_copy(out=b_all[:, :, (c - 16) * 128:(c - 15) * 128],
                                                                in_=tp[:, 0:512].rearrange("p (s t) -> p s t", s=NST)),
                                 reads=[rtp], writes=r_ball)
                    else:
                        S.op("act", lambda e: e.activation(out=CT[:, c - 20, :], in_=acc[:, 0:512], func=AF.Silu),
                             reads=[racc], writes=[r_CT[c - 20]])

            pend = [p4_s1(0), p4_s1(1)]
            for pp in range(12):
                nxt = [p4_s1(2 * pp + 2), p4_s1(2 * pp + 3)] if pp < 11 else []
                p4_s2(pend)
                pend = nxt

            def z_unit(sti, q):
                def f():
                    wv, rw = wload(win_v[:, :, q * 512:(q + 1) * 512], 8, 512, [r_win_parts[0]])
                    bk, rbk = getbank()
                    for kc in range(8):
                        S.op("pe", lambda e, kc=kc: e.matmul(bk[:], lhsT=hnT[:, kc, sti * 128:(sti + 1) * 128],
                                                             rhs=wv[:, kc, :], start=(kc == 0), stop=(kc == 7)),
                             reads=[rw, r_hnT[kc]], writes=[rbk])
                    S.op("act", lambda e: e.activation(out=zs[sti % 2][:, q * 512:(q + 1) * 512], in_=bk[:], func=AF.Silu),
                         reads=[rbk], writes=[r_zs[sti % 2]])
                return f

            def sconv_unit(j):
                def f():
                    wv, rw = wload(win_v[:, :, 5120 + j * 384:5120 + (j + 1) * 384], 8, 384, r_win_parts[3:6])
                    pb, rpb = proj_fm(wv, rw, 0, 8, lambda kc: hnT[:, kc, :], r_hnT)
                    pc, rpc = proj_fm(wv, rw, 128, 8, lambda kc: hnT[:, kc, :], r_hnT)
                    px, rpx = proj_fm(wv, rw, 256, 8, lambda kc: hnT[:, kc, :], r_hnT)
                    sc_, rsc = gettmp()
                    S.op("act", lambda e: e.activation(out=sc_[:, 0:512], in_=pc[:], func=AF.Copy), reads=[rpc], writes=[rsc])
                    prd, rprd = gettmp()
                    S.op("dve", lambda e: e.tensor_tensor(out=prd[:, 2:514], in0=px[:], in1=sc_[:, 0:512], op=ALU.mult),
                         reads=[rpx, rsc], writes=[rprd])
                    S.op("pool", lambda e: e.tensor_copy(out=prd[:, 0:2], in_=halo2[:, j, :]), reads=[r_halo2], writes=[rprd])
                    S.op("pool", lambda e: e.tensor_copy(out=halo2[:, j, :], in_=prd[:, 512:514]), reads=[rprd], writes=[r_halo2])
                    acc, racc = gettmp()
                    S.op("dve", lambda e: e.tensor_scalar(out=acc[:, 0:512], in0=prd[:, 0:512], scalar1=pt2[:, 32 + j:33 + j],
                                                          scalar2=None, op0=ALU.mult),
                         reads=[rprd] + RC, writes=[racc])
                    for k in range(1, 3):
                        S.op("dve", lambda e, k=k: e.scalar_tensor_tensor(
                            out=acc[:, 0:512], in0=prd[:, k:k + 512], scalar=pt2[:, 32 + k * 8 + j:33 + k * 8 + j],
                            in1=acc[:, 0:512], op0=ALU.mult, op1=ALU.add),
                            reads=[rprd, racc] + RC, writes=[racc])
                    S.op("dve", lambda e: e.tensor_tensor(out=uT[:, j, :], in0=acc[:, 0:512], in1=pb[:], op=ALU.mult),
                         reads=[racc, rpb], writes=[r_uT[j]])
                return f

            def gate_unit(blk):
                def f():
                    wv, rw = wload(win_v[:, :, 8192 + blk * 512:8192 + (blk + 1) * 512], 8, 512, [r_win_parts[6]])
                    for j4 in range(4):
                        gi = blk * 4 + j4
                        bk, rbk = proj_fm(wv, rw, j4 * 128, 8, lambda kc: hnT[:, kc, :], r_hnT)
                        S.op("act", lambda e: e.activation(out=gsig[:, gi, :], in_=bk[:], func=AF.Sigmoid),
                             reads=[rbk], writes=[r_gsig[gi]])
                return f

            def V(i):
                return sm[:, i * 128:(i + 1) * 128]

            def V3(i):
                return V(i).rearrange("p (s h) -> p s h", s=NST)

            xsd_buf = [xsd, xdt]
            r_xsd_buf = [r_xsd, r_xdt]

            def emit_xsd(st2):
                S.op("pool", lambda e: e.tensor_tensor(
                    out=xsd_buf[st2 % 2][:].rearrange("p (h d) -> p h d", h=32),
                    in0=xs_all[:, st2, :].rearrange("p (h d) -> p h d", h=32),
                    in1=bc[:, 2, :].unsqueeze(2).to_broadcast([128, 32, 64]), op=ALU.mult),
                    reads=[r_xs[st2]] + RC, writes=[r_xsd_buf[st2 % 2]])

            def ssd_gen():
                bk, rbk = getbank()
                for sti in range(NST):
                    for kc in range(8):
                        S.op("pe", lambda e, kc=kc, sti=sti: e.matmul(bk[:, sti * 32:(sti + 1) * 32],
                                                                      lhsT=hnT[:, kc, sti * 128:(sti + 1) * 128],
                                                                      rhs=wdt[:, kc, :], start=(kc == 0), stop=(kc == 7)),
                             reads=[r_hnT[kc]] + RC, writes=[rbk])

                def b4(v):
                    return v.unsqueeze(1).to_broadcast([128, NST, 32])
                S.op("dve", lambda e: e.tensor_tensor(out=V3(0), in0=bk[:, 0:128].rearrange("p (s h) -> p s h", s=NST),
                                                      in1=b4(bc[:, 0, :]), op=ALU.add),
                     reads=[rbk] + RC, writes=[r_sm])
                S.op("act", lambda e: e.activation(out=V(0), in_=V(0), func=AF.Exp), reads=[r_sm], writes=[r_sm])
                S.op("act", lambda e: e.activation(out=V(1), in_=V(0), func=AF.Ln, bias=1.0), reads=[r_sm], writes=[r_sm])
                S.op("act", lambda e: e.activation(out=V(11), in_=V(1), func=AF.Ln), reads=[r_sm], writes=[r_sm])
                S.op("dve", lambda e: e.tensor_tensor(out=V3(2), in0=V3(1), in1=b4(bc[:, 1, :]), op=ALU.mult),
                     reads=[r_sm] + RC, writes=[r_sm])
                dh0 = dhl[:, 0, :, :].rearrange("p s h -> p (s h)")
                dh1 = dhl[:, 1, :, :].rearrange("p s h -> p (s h)")
                S.op("dve", lambda e: e.tensor_copy(out=dh0, in_=V(2)), reads=[r_sm], writes=[r_dhl])
                S.op("dve", lambda e: e.tensor_tensor(out=V(3), in0=V(2), in1=dh0, op=ALU.subtract),
                     reads=[r_sm, r_dhl], writes=[r_sm])
                S.op("dve", lambda e: e.tensor_copy(out=dh1, in_=V(3)), reads=[r_sm], writes=[r_dhl])
                bk2, rbk2 = getbank()
                for sti in range(NST):
                    for (c0, lhs) in ((0, tri_le), (32, ones_b)):
                        for hl in range(2):
                            S.op("pe", lambda e, c0=c0, lhs=lhs, hl=hl, sti=sti: e.matmul(
                                bk2[:, sti * 64 + c0:sti * 64 + c0 + 32], lhsT=lhs[:], rhs=dhl[:, hl, sti, :],
                                start=(hl == 0), stop=(hl == 1)),
                                reads=[r_dhl] + RC, writes=[rbk2])
                S.op("dve", lambda e: e.tensor_copy(out=sm[:, 512:768], in_=bk2[:, 0:256]), reads=[rbk2], writes=[r_sm])
                at3 = sm[:, 512:768].rearrange("p (s c) -> p s c", s=NST)
                acs3, tots3 = at3[:, :, 0:32], at3[:, :, 32:64]
                S.op("dve", lambda e: e.tensor_tensor(out=V3(6), in0=tots3, in1=acs3, op=ALU.subtract), reads=[r_sm], writes=[r_sm])
                S.op("act", lambda e: e.activation(out=V3(7), in_=acs3, func=AF.Exp), reads=[r_sm], writes=[r_sm])
                S.op("act", lambda e: e.activation(out=V3(8), in_=tots3, func=AF.Exp), reads=[r_sm], writes=[r_sm])
                S.op("act", lambda e: e.activation(out=V(9), in_=V(6), func=AF.Exp), reads=[r_sm], writes=[r_sm])
                S.op("dve", lambda e: e.tensor_tensor(out=V(10), in0=V(1), in1=V(9), op=ALU.mult), reads=[r_sm], writes=[r_sm])
                yield ("pt", -1)

                for sti in range(NST):
                    tsl = slice(sti * 128, (sti + 1) * 128)
                    dt_s, eac, etot, w2 = V3(1)[:, sti, :], V3(7)[:, sti, :], V3(8)[:, sti, :], V3(10)[:, sti, :]
                    zcur, rz = zs[sti % 2], r_zs[sti % 2]
                    xs3 = xs_all[:, sti, :].rearrange("p (h d) -> p h d", h=32)

                    def bch(v):
                        return v.unsqueeze(2).to_broadcast([128, 32, 64])
                    xsd_c, r_xsd_c = xsd_buf[sti % 2], r_xsd_buf[sti % 2]
                    lndt_s = V3(11)[:, sti, :]
                    bkc, rbkc = getbank()
                    for g in range(4):
                        S.op("pe", lambda e, g=g: e.matmul(bkc[:, g * 128:(g + 1) * 128], lhsT=BT[:, g, tsl], rhs=CT[:, g, tsl],
                                                           start=True, stop=True),
                             reads=[r_BT[g], r_CT[g]], writes=[rbkc])
                    S.op("dve", lambda e: e.tensor_tensor(out=cbm[:], in0=bkc[:].rearrange("p (g l) -> p g l", g=4),
                                                          in1=tri_le[:].unsqueeze(1).to_broadcast([128, 4, 128]), op=ALU.mult),
                         reads=[rbkc] + RC, writes=[r_cbm])
                    yield ("pt", sti)

                    def stage_a(g):
                        R_, rR = Rg[g % 2], r_Rg[g % 2]
                        E_, rE = Eg[g % 2], r_Eg[g % 2]
                        S.op("pool", lambda e: e.tensor_tensor(
                            out=R_[:], in0=tri_le[:].unsqueeze(1).to_broadcast([128, 8, 128]),
                            in1=dhl[:, 0, sti, g * 8:(g + 1) * 8].unsqueeze(2).to_broadcast([128, 8, 128]), op=ALU.mult),
                            reads=[r_dhl] + RC, writes=[rR])
                        for half in range(2):
                            bs, rbs = getbank()
                            S.op("pe", lambda e: e.matmul(bs[:], lhsT=mstrict[:], rhs=R_[:, half * 4:(half + 1) * 4, :],
                                                          start=True, stop=True),
                                 reads=[rR] + RC, writes=[rbs])
                            for r4 in range(4):
                                hh = g * 8 + half * 4 + r4
                                S.op("act", lambda e, r4=r4, hh=hh: e.activation(out=E_[:, half * 4 + r4, :],
                                                                                 in_=bs[:, r4 * 128:(r4 + 1) * 128], func=AF.Exp,
                                                                                 bias=lndt_s[:, hh:hh + 1]),
                                     reads=[rbs, r_sm], writes=[rE])
                        S.op("dve", lambda e: e.tensor_tensor(out=scr_[g % 2][:], in0=E_[:],
                                                              in1=cbm[:, g, :].unsqueeze(1).to_broadcast([128, 8, 128]), op=ALU.mult),
                             reads=[rE, r_cbm], writes=[r_scr[g % 2]])

                    def stage_b(g):
                        sc, rsc = scr_[g % 2], r_scr[g % 2]
                        yd, ryd = getbank()
                        S.op("pe", lambda e: e.matmul(yd[:], lhsT=ident_b[:], rhs=xsd_c[:, g * 512:(g + 1) * 512], start=True, stop=False),
                             reads=[r_xsd_c] + RC, writes=[ryd])
                        for r in range(8):
                            S.op("pe", lambda e, r=r: e.matmul(yd[:, r * 64:(r + 1) * 64], lhsT=sc[:, r, :],
                                                               rhs=xs_all[:, sti, (g * 8 + r) * 64:(g * 8 + r + 1) * 64],
                                                               start=False, stop=(r == 7)),
                                 reads=[rsc, r_xs[sti]], writes=[ryd])
                        yo, ryo = getbank()
                        S.op("pe", lambda e: e.matmul(yo[:], lhsT=CT[:, g, tsl], rhs=stateb[:, g, :], start=True, stop=True),
                             reads=[r_CT[g], r_stateb[g]], writes=[ryo])
                        yg = yg4[:, g * 512:(g + 1) * 512]
                        S.op("dve", lambda e: e.tensor_tensor(out=yg.rearrange("p (r d) -> p r d", r=8),
                                                              in0=yo[:].rearrange("p (r d) -> p r d", r=8),
                                                              in1=eac[:, g * 8:(g + 1) * 8].unsqueeze(2).to_broadcast([128, 8, 64]),
                                                              op=ALU.mult),
                             reads=[ryo, r_sm], writes=[r_yg4[g]])
                        S.op("dve", lambda e: e.tensor_tensor(out=yg, in0=yg, in1=yd[:], op=ALU.add),
                             reads=[ryd, r_yg4[g]], writes=[r_yg4[g]])
                        S.op("dve", lambda e: e.tensor_tensor(out=yg, in0=yg, in1=zcur[:, g * 512:(g + 1) * 512], op=ALU.mult),
                             reads=[r_yg4[g], rz], writes=[r_yg4[g]])
                        jk, rjk = gettmp()
                        S.op("act", lambda e: e.activation(out=jk[:, 0:512], in_=yg, func=AF.Square,
                                                           accum_out=stat[:, 16 + g:17 + g]),
                             reads=[r_yg4[g]], writes=[rjk, r_stat])

                    stage_a(0)
                    yield ("pt", sti)
                    for g in range(4):
                        if g + 1 < 4:
                            stage_a(g + 1)
                            yield ("pt", sti)
                        if g == 2:
                            S.op("pool", lambda e: e.tensor_tensor(out=xdtw[:].rearrange("p (h d) -> p h d", h=32), in0=xs3, in1=bch(w2), op=ALU.mult),
                                 reads=[r_xs[sti], r_sm], writes=[r_xdtw])
                            if sti + 1 < NST:
                                emit_xsd(sti + 1)
                        stage_b(g)
                        yield ("pt", sti)
                    S.op("act", lambda e: e.activation(out=stat[:, 20:24], in_=stat[:, 16:20], func=AF.Ln, scale=1.0 / 512, bias=EPS),
                         reads=[r_stat], writes=[r_stat])
                    S.op("act", lambda e: e.activation(out=stat[:, 24:28], in_=stat[:, 20:24], func=AF.Exp, scale=-0.5),
                         reads=[r_stat], writes=[r_stat])
                    for g in range(4):
                        yb, ryb = ygb[g % 2], r_ygb[g % 2]
                        S.op("act", lambda e: e.activation(out=yb[:], in_=yg4[:, g * 512:(g + 1) * 512], func=AF.Copy,
                                                           scale=stat[:, 24 + g:25 + g]),
                             reads=[r_yg4[g], r_stat], writes=[ryb])
                        tp, rtp = gettp()
                        for cc in range(4):
                            S.op("pe", lambda e, cc=cc: e.transpose(out=tp[:, cc * 128:(cc + 1) * 128],
                                                                    in_=yb[:, cc * 128:(cc + 1) * 128], identity=ident_b[:]),
                                 reads=[ryb] + RC, writes=[rtp])
                        S.op("dve", lambda e: e.tensor_tensor(
                            out=yT[:, g * 4:(g + 1) * 4, tsl], in0=tp[:, 0:512].rearrange("p (c t) -> p c t", c=4),
                            in1=pt2[:, 16 + g * 4:16 + (g + 1) * 4].unsqueeze(2).to_broadcast([128, 4, 128]), op=ALU.mult),
                            reads=[rtp] + RC, writes=r_yT[g * 4:(g + 1) * 4])
                        yield ("pt", sti)
                    for g in range(4):
                        sn, rsn = getbank()
                        S.op("pe", lambda e: e.matmul(sn[:], lhsT=b_all[:, sti, g * 128:(g + 1) * 128], rhs=xdtw[:, g * 512:(g + 1) * 512],
                                                      start=True, stop=True),
                             reads=[r_ball[sti], r_xdtw], writes=[rsn])
                        S.op("dve", lambda e: e.tensor_tensor(out=state[:, g, :].rearrange("p (r d) -> p r d", r=8),
                                                              in0=state[:, g, :].rearrange("p (r d) -> p r d", r=8),
                                                              in1=etot[:, g * 8:(g + 1) * 8].unsqueeze(2).to_broadcast([128, 8, 64]),
                                                              op=ALU.mult),
                             reads=[r_state[g], r_sm], writes=[r_state[g]])
                        S.op("dve", lambda e: e.tensor_tensor(out=state[:, g, :], in0=state[:, g, :], in1=sn[:], op=ALU.add),
                             reads=[r_state[g], rsn], writes=[r_state[g]])
                        S.op("act", lambda e: e.activation(out=stateb[:, g, :], in_=state[:, g, :], func=AF.Copy),
                             reads=[r_state[g]], writes=[r_stateb[g]])
                        yield ("pt", sti)
                    yield ("end", sti)

            others = [sconv_unit(j) for j in range(8)] + [gate_unit(b) for b in range(4)]
            fill = []
            for sti in range(NST):
                f_ = [z_unit(sti + 1, q) for q in range(4)] if sti + 1 < NST else []
                fill.append(f_ + others[sti * 3:(sti + 1) * 3])
            emit_xsd(0)
            for q in range(4):
                z_unit(0, q)()
            npt = 0
            for kind, sti in ssd_gen():
                if sti < 0:
                    continue
                if kind == "pt":
                    npt += 1
                    if late_casts:
                        late_casts.pop(0)()
                    if npt % 2 == 0 and fill[sti]:
                        fill[sti].pop(0)()
                else:
                    while fill[sti]:
                        fill[sti].pop(0)()
            while late_casts:
                late_casts.pop(0)()
            dbg('yT', yT[:], r_yT, ti)
            dbg('state', state[:], r_state, ti)
            dbg('uT', uT, r_uT, ti)
            dbg('gsig', gsig, r_gsig, ti)

            wssd_v = wssd_s.rearrange("(kc p) n -> p kc n", p=128)
            wsc_v = wsc_s.rearrange("(kc p) n -> p kc n", p=128)
            for ob in range(2):
                wa0, rwa0 = wload(wssd_v[:, 0:8, ob * 512:(ob + 1) * 512], 8, 512, r_wssd)
                wa1, rwa1 = wload(wssd_v[:, 8:16, ob * 512:(ob + 1) * 512], 8, 512, r_wssd)
                wb, rwb = wload(wsc_v[:, :, ob * 512:(ob + 1) * 512], 8, 512, r_wsc)
                for j4 in range(4):
                    oc = ob * 4 + j4
                    pa, rpa = getbank()
                    for kc in range(16):
                        wv_, rw_ = (wa0, rwa0) if kc < 8 else (wa1, rwa1)
                        S.op("pe", lambda e, kc=kc, wv_=wv_: e.matmul(pa[:], lhsT=wv_[:, kc % 8, j4 * 128:(j4 + 1) * 128],
                                                                      rhs=yT[:, kc, :], start=(kc == 0), stop=(kc == 15)),
                             reads=[rw_, r_yT[kc]], writes=[rpa])
                    pb, rpb = proj_fm(wb, rwb, j4 * 128, 8, lambda kc: uT[:, kc, :], r_uT)
                    m1, rm1 = gettmp()
                    m2, rm2 = gettmp()
                    S.op("dve", lambda e: e.tensor_tensor(out=m1[:, 0:512], in0=pa[:], in1=gsig[:, oc, :], op=ALU.mult),
                         reads=[rpa, r_gsig[oc]], writes=[rm1])
                    S.op("dve", lambda e: e.tensor_tensor(out=m2[:, 0:512], in0=pb[:], in1=gsig[:, 8 + oc, :], op=ALU.mult),
                         reads=[rpb, r_gsig[8 + oc]], writes=[rm2])
                    S.op("dve", lambda e: e.tensor_tensor(out=hnT[:, oc, :], in0=m1[:, 0:512], in1=m2[:, 0:512], op=ALU.add),
                         reads=[rm1, rm2], writes=[r_hnT[oc]])

            dbg('mT', hnT[:], r_hnT, ti)
            wo_v = wo_s.rearrange("(kc p) n -> p kc n", p=128)
            for ob in range(2):
                wv, rw = wload(wo_v[:, :, ob * 512:(ob + 1) * 512], 8, 512, r_wo)
                for sti in range(NST):
                    bk, rbk = getbank()
                    for kc in range(8):
                        S.op("pe", lambda e, kc=kc: e.matmul(bk[:], lhsT=hnT[:, kc, sti * 128:(sti + 1) * 128], rhs=wv[:, kc, :],
                                                             start=(kc == 0), stop=(kc == 7)),
                             reads=[rw, r_hnT[kc]], writes=[rbk])
                    S.op("dve", lambda e: e.tensor_tensor(out=xres[:, sti, ob * 512:(ob + 1) * 512],
                                                          in0=xres[:, sti, ob * 512:(ob + 1) * 512], in1=bk[:], op=ALU.add),
                         reads=[rbk, r_xres[sti]], writes=[r_xres[sti]])

            dbg('h1', xres[:], r_xres, ti)
            for sti in range(NST):
                rmsnorm_to_T(sti, 8, "ffn")

            wg_v = wg_s.rearrange("(kc p) n -> p kc n", p=128)
            wu_v = wu_s.rearrange("(kc p) n -> p kc n", p=128)
            for blk in range(6):
                ncol = 512 if blk < 5 else 256
                wgv, rwg = wload(wg_v[:, :, blk * 512:blk * 512 + ncol], 8, ncol, r_wg)
                wuv, rwu = wload(wu_v[:, :, blk * 512:blk * 512 + ncol], 8, ncol, r_wu)
                for j4 in range(ncol // 128):
                    f = blk * 4 + j4
                    pg, rpg = proj_fm(wgv, rwg, j4 * 128, 8, lambda kc: hnT[:, kc, :], r_hnT)
                    pu, rpu = proj_fm(wuv, rwu, j4 * 128, 8, lambda kc: hnT[:, kc, :], r_hnT)
                    sg, rsg = gettmp()
                    S.op("act", lambda e: e.activation(out=sg[:, 0:512], in_=pg[:], func=AF.Silu), reads=[rpg], writes=[rsg])
                    S.op("dve", lambda e: e.tensor_tensor(out=actT[:, f, :], in0=sg[:, 0:512], in1=pu[:], op=ALU.mult),
                         reads=[rsg, rpu], writes=[r_actT[f]])

            dbg('actT', actT, r_actT, ti)
            wd_v = wd_s.rearrange("(kc p) n -> p kc n", p=128)
            for ob in range(2):
                accs = [getbank() for _ in range(NST)]
                for (k0, k1) in ((0, 8), (8, 16), (16, 22)):
                    wv_, rw_ = wload(wd_v[:, k0:k1, ob * 512:(ob + 1) * 512], k1 - k0, 512, r_wd)
                    for sti in range(NST):
                        bk, rbk = accs[sti]
                        for kc in range(k0, k1):
                            S.op("pe", lambda e, kc=kc: e.matmul(bk[:], lhsT=actT[:, kc, sti * 128:(sti + 1) * 128],
                                                                 rhs=wv_[:, kc - k0, :], start=(kc == 0), stop=(kc == 21)),
                                 reads=[rw_, r_actT[kc]], writes=[rbk])
                for sti in range(NST):
                    bk, rbk = accs[sti]
                    S.op("dve", lambda e: e.tensor_tensor(out=xres[:, sti, ob * 512:(ob + 1) * 512],
                                                          in0=xres[:, sti, ob * 512:(ob + 1) * 512], in1=bk[:], op=ALU.add),
                         reads=[rbk, r_xres[sti]], writes=[r_xres[sti]])
            dbg('h2', xres[:], r_xres, ti)

            for sti in range(NST):
                S.op("act", lambda e: e.activation(out=hnb[:], in_=xres[:, sti, :], func=AF.Square, accum_out=stat[:, 8:9]),
                     reads=[r_xres[sti]], writes=[r_hnb, r_stat])
                S.op("act", lambda e: e.activation(out=stat[:, 9:10], in_=stat[:, 8:9], func=AF.Ln, scale=1.0 / D, bias=EPS),
                     reads=[r_stat], writes=[r_stat])
                S.op("act", lambda e: e.activation(out=stat[:, 10:11], in_=stat[:, 9:10], func=AF.Exp, scale=-0.5),
                     reads=[r_stat], writes=[r_stat])
                ob = yg4[:, (sti % 2) * 1024:(sti % 2 + 1) * 1024]
                rob = r_yg4[(sti % 2) * 2:(sti % 2) * 2 + 2]
                S.op("dve", lambda e: e.scalar_tensor_tensor(out=ob, in0=xres[:, sti, :], scalar=stat[:, 10:11],
                                                             in1=fnw[:], op0=ALU.mult, op1=ALU.mult),
                     reads=[r_xres[sti], r_stat] + RC, writes=rob)
                S.dma("pool", "out", out_d[t0 + sti * 128:t0 + (sti + 1) * 128, :], ob, reads=rob)

        S.wait_all("pool", r_yg4)
        S.wait_all("sp", r_yg4)
    return nc, S


_NAMES = ["x", "norm_mix_w", "w_in", "ssd_conv_w", "ssd_conv_b", "dt_bias", "a_log", "d_skip", "ssd_norm_w",
          "w_ssd_proj", "sconv_w", "w_sconv_proj", "w_o", "norm_ffn_w", "w_gate", "w_up", "w_down", "final_norm_w"]


def kernel(**inputs):
    x = np.asarray(inputs["x"], dtype=np.float32)
    B, T, _ = x.shape
    nc, _ = build(T)
    shared = {k: np.ascontiguousarray(np.asarray(inputs[k], dtype=np.float32)) for k in _NAMES if k != "x"}
    in_maps = []
    for b in range(B):
        m = dict(shared)
        m["x"] = np.ascontiguousarray(x[b])
        in_maps.append(m)
    res = run_bass_kernel_spmd(nc, in_maps, core_ids=list(range(B)))
    out = np.stack([np.asarray(r["out"]) for r in res.results], axis=0)
    return out.astype(np.float32)
```

```python
import numpy as np
from contextlib import ExitStack
import concourse.bass as bass
import concourse.mybir as mybir
from concourse.bass_utils import run_bass_kernel_spmd

F32 = mybir.dt.float32
BF16 = mybir.dt.bfloat16
ALU = mybir.AluOpType
AF = mybir.ActivationFunctionType

D = 1024
TT = 512
NST = 4
DIN = 10272
FF = 2816
EPS = 1e-6
WSLOT = 4096
NWSLOT = 3


class Res:
    __slots__ = ("name", "writes", "reads", "overlaps")

    def __init__(self, name):
        self.name = name
        self.writes = {}
        self.reads = {}
        self.overlaps = []


class Sched:
    def __init__(self, nc, stack, same_engine_sync=False):
        self.nc = nc
        self.stack = stack
        self.engs = {"pe": nc.tensor, "act": nc.scalar, "dve": nc.vector,
                     "pool": nc.gpsimd, "sp": nc.sync}
        self.sem, self.count, self.known, self.snap = {}, {}, {}, {}
        self.same = same_engine_sync
        for k in self.engs:
            self.sem[k] = stack.enter_context(nc.semaphore("prog_" + k))
            self.count[k] = 0
            self.known[k] = {}
            self.snap[k] = [None]
        self.dma_sems = {}
        self.nwaits = 0
        self.ninstr = 0

    def dma_sem(self, name):
        if name not in self.dma_sems:
            s = self.stack.enter_context(self.nc.semaphore("dma_" + name))
            self.dma_sems[name] = s
            self.sem[name] = s
            self.count[name] = 0
            self.snap[name] = {}
        return name

    SAME_SYNC = ("act", "dve", "pool")

    def _need(self, eng, deps):
        raw, other = deps
        kn = self.known[eng]
        allk = dict(other)
        for k, v in raw.items():
            if allk.get(k, 0) < v:
                allk[k] = v
        for key, val in allk.items():
            if key == eng:
                if not (eng in self.SAME_SYNC and raw.get(key, 0) > kn.get(key, 0)):
                    continue
                val = raw[key]
            if kn.get(key, 0) >= val:
                continue
            self.engs[eng].wait_ge(self.sem[key], val)
            self.nwaits += 1
            kn[key] = val
            sn = self.snap[key]
            s = sn[val] if isinstance(sn, list) else sn.get(val)
            if s:
                for k2, v2 in s.items():
                    if kn.get(k2, 0) < v2:
                        kn[k2] = v2

    @staticmethod
    def _deps(reads, writes):
        raw, other = {}, {}

        def add(t, d):
            for k, v in d.items():
                if t.get(k, 0) < v:
                    t[k] = v
        for r in reads:
            add(raw, r.writes)
        for w in writes:
            add(other, w.writes)
            add(other, w.reads)
            for o in w.overlaps:
                add(other, o.writes)
                add(other, o.reads)
        return raw, other

    def op(self, eng, fn, reads=(), writes=()):
        self._need(eng, self._deps(reads, writes))
        ins = fn(self.engs[eng])
        self.count[eng] += 1
        c = self.count[eng]
        ins.then_inc(self.sem[eng], 1)
        self.ninstr += 1
        self.snap[eng].append(dict(self.known[eng]))
        for r in reads:
            if r.reads.get(eng, 0) < c:
                r.reads[eng] = c
        for w in writes:
            w.writes = {eng: c}
            w.reads = {}

    def dma(self, queue, semname, out, in_, reads=(), writes=(), **kw):
        self.dma_sem(semname)
        self._need(queue, self._deps(reads, writes))
        ins = self.engs[queue].dma_start(out=out, in_=in_, **kw)
        self.count[semname] += 16
        c = self.count[semname]
        ins.then_inc(self.sem[semname], 16)
        self.ninstr += 1
        self.snap[semname][c] = dict(self.known[queue])
        for r in reads:
            if r.reads.get(semname, 0) < c:
                r.reads[semname] = c
        for w in writes:
            w.writes = {semname: c}
            w.reads = {}

    def wait_all(self, eng, resources):
        deps = {}
        for r in resources:
            for d in (r.writes, r.reads):
                for k, v in d.items():
                    if deps.get(k, 0) < v:
                        deps[k] = v
        self._need(eng, ({}, deps))


def build(T, debug=False):
    NT = T // TT
    nc = bass.Bass("TRN2", target_bir_lowering=False)

    def din(name, shape):
        return nc.dram_tensor(name, list(shape), F32, kind="ExternalInput").ap()

    x_d = din("x", [T, D])
    nmw_d = din("norm_mix_w", [1, D])
    win_d = din("w_in", [1, D, DIN])
    cw_d = din("ssd_conv_w", [1, 4, 3072])
    cb_d = din("ssd_conv_b", [1, 3072])
    dtb_d = din("dt_bias", [1, 32])
    alog_d = din("a_log", [1, 32])
    dsk_d = din("d_skip", [1, 32])
    snw_d = din("ssd_norm_w", [1, 2048])
    wssd_d = din("w_ssd_proj", [1, 2048, D])
    scw_d = din("sconv_w", [1, 3, D])
    wsc_d = din("w_sconv_proj", [1, D, D])
    wo_d = din("w_o", [1, D, D])
    nfw_d = din("norm_ffn_w", [1, D])
    wg_d = din("w_gate", [1, D, FF])
    wu_d = din("w_up", [1, D, FF])
    wd_d = din("w_down", [1, FF, D])
    fnw_d = din("final_norm_w", [D])
    out_d = nc.dram_tensor("out", [T, D], F32, kind="ExternalOutput").ap()

    def dscr(name, shape):
        return nc.dram_tensor(name, list(shape), BF16, kind="Internal").ap()

    win_s = dscr("win_s", [D, DIN])
    wssd_s = dscr("wssd_s", [2048, D])
    wsc_s = dscr("wsc_s", [D, D])
    wo_s = dscr("wo_s", [D, D])
    wg_s = dscr("wg_s", [D, FF])
    wu_s = dscr("wu_s", [D, FF])
    wd_s = dscr("wd_s", [FF, D])

    with ExitStack() as st_:
        S = Sched(nc, st_)

        def sb(name, shape, dt):
            return st_.enter_context(nc.sbuf_tensor(name, list(shape), dt))

        def ps(name, shape, dt):
            return st_.enter_context(nc.psum_tensor(name, list(shape), dt))

        r_win, r_wssd, r_wsc, r_wo, r_wg, r_wu, r_wd = (Res(n) for n in
                                                        ("win", "wssd", "wsc", "wo", "wg", "wu", "wd"))
        wslots = [sb("wslot%d" % i, [128, WSLOT], BF16) for i in range(NWSLOT)]
        r_wslot = [Res("wslot%d" % i) for i in range(NWSLOT)]
        xres = sb("xres", [128, NST, D], F32)
        r_xres = [Res("xres%d" % i) for i in range(NST)]
        hnb = sb("hnb", [128, D], BF16)
        r_hnb = Res("hnb")
        hnT = sb("hnT", [128, 8, TT], BF16)
        r_hnT = [Res("hnT%d" % i) for i in range(8)]
        arena = sb("arena", [128, 24576], BF16)
        xs_all = arena[:, 0:8192].rearrange("p (s c) -> p s c", s=NST)
        BT = arena[:, 8192:10240].rearrange("p (g t) -> p g t", g=4)
        CT = arena[:, 10240:12288].rearrange("p (g t) -> p g t", g=4)
        uT = arena[:, 12288:16384].rearrange("p (j t) -> p j t", j=8)
        gsig = arena[:, 16384:24576].rearrange("p (j t) -> p j t", j=16)
        actT = arena[:, 0:11264].rearrange("p (j t) -> p j t", j=22)
        zs = [sb("zs%d" % i, [128, 2048], BF16) for i in range(2)]
        r_zs = [Res("zs%d" % i) for i in range(2)]
        r_xs = [Res("xs%d" % i) for i in range(NST)]
        r_BT = [Res("BT%d" % i) for i in range(4)]
        r_CT = [Res("CT%d" % i) for i in range(4)]
        r_uT = [Res("uT%d" % i) for i in range(8)]
        r_gsig = [Res("gsig%d" % i) for i in range(16)]
        r_actT = [Res("actT%d" % i) for i in range(22)]
        ph_ssd = r_xs + r_BT + r_CT
        ph_ffn = r_actT
        for r in ph_ssd:
            r.overlaps = ph_ffn
        for r in ph_ffn:
            r.overlaps = ph_ssd
        b_all = sb("b_all", [128, NST, 512], BF16)
        r_ball = [Res("ball%d" % i) for i in range(NST)]
        yT = sb("yT", [128, 16, TT], BF16)
        r_yT = [Res("yT%d" % i) for i in range(16)]
        state = sb("state", [128, 4, 512], F32)
        stateb = sb("stateb", [128, 4, 512], BF16)
        r_state = [Res("state%d" % i) for i in range(4)]
        r_stateb = [Res("stateb%d" % i) for i in range(4)]
        xdt = sb("xdt", [128, 2048], BF16)
        xdtw = sb("xdtw", [128, 2048], BF16)
        xsd = sb("xsd", [128, 2048], BF16)
        r_xdt, r_xdtw, r_xsd = Res("xdt"), Res("xdtw"), Res("xsd")
        cbm = sb("cbm", [128, 4, 128], BF16)
        r_cbm = Res("cbm")
        Rg = [sb("Rg%d" % i, [128, 8, 128], BF16) for i in range(2)]
        r_Rg = [Res("Rg%d" % i) for i in range(2)]
        Eg = [sb("Eg%d" % i, [128, 8, 128], BF16) for i in range(2)]
        r_Eg = [Res("Eg%d" % i) for i in range(2)]
        scr_ = [sb("scores%d" % i, [128, 8, 128], BF16) for i in range(2)]
        r_scr = [Res("scores%d" % i) for i in range(2)]
        yg4 = sb("yg4", [128, 2048], F32)
        r_yg4 = [Res("yg4_%d" % i) for i in range(4)]
        ygb = [sb("ygb0", [128, 512], BF16)] * 2
        r_ygb = [Res("ygb0")] * 2
        NTMP = 9
        tmpf = [sb("tmpf%d" % i, [128, 516], F32) for i in range(NTMP)]
        r_tmpf = [Res("tmpf%d" % i) for i in range(NTMP)]
        tmp_i = [0]

        def gettmp():
            i = tmp_i[0] % NTMP
            tmp_i[0] += 1
            return tmpf[i], r_tmpf[i]
        xsT = [sb("xsT%d" % i, [128, 512], BF16) for i in range(2)]
        r_xsT = [Res("xsT%d" % i) for i in range(2)]
        halo = sb("halo", [128, 24, 3], F32)
        r_halo = Res("halo")
        halo2 = sb("halo2", [128, 8, 2], F32)
        r_halo2 = Res("halo2")
        sm = sb("sm", [128, 1536], F32)
        r_sm = Res("sm")
        dhl = sb("dhl", [128, 2, 4, 32], BF16)
        r_dhl = Res("dhl")
        stat = sb("stat", [128, 32], F32)
        r_stat = Res("stat")
        fnw = sb("fnw", [128, D], F32)
        tri_le = sb("tri_le", [128, 128], BF16)
        mstrict = sb("mstrict", [128, 128], BF16)
        ones_b = sb("ones_b", [128, 128], BF16)
        ident_b = sb("ident_b", [128, 128], BF16)
        ident_f = tmpf[3][:, 0:128]
        mk_f = tmpf[2][:, 0:128]
        pin1 = tmpf[0][:, 0:128]
        pin2 = tmpf[1][:, 0:128]
        pt1 = sb("pt1", [128, 120], F32)
        pt2 = sb("pt2", [128, 56], F32)
        bc = sb("bc", [128, 4, 32], F32)
        wdt = sb("wdt", [128, 8, 32], BF16)
        r_const = Res("const")
        r_mk = Res("mk")

        banks = [ps("bank%d" % i, [128, 512], F32) for i in (0, 1, 2)]
        tpb = [ps("tp0", [128, 1024], BF16)]
        banks += [ps("bank%d" % i, [128, 512], F32) for i in (4, 5, 6)]
        tpb += [ps("tp1", [128, 1024], BF16)]
        r_bank = [Res("bank%d" % i) for i in range(6)]
        r_tpb = [Res("tp%d" % i) for i in range(2)]
        mm_i = [0]

        def getbank():
            i = mm_i[0] % 6
            mm_i[0] += 1
            return banks[i], r_bank[i]
        tp_i = [0]

        def gettp():
            i = tp_i[0] % 2
            tp_i[0] += 1
            return tpb[i], r_tpb[i]

        dbg_n = [0]

        def dbg(name, ap, reads, ti=0):
            if not debug or ti != 0:
                return
            dt_ = nc.dram_tensor("dbg_" + name, list(ap.shape), ap.dtype, kind="ExternalOutput").ap()
            S.dma("sp", "dbg", dt_, ap, reads=reads)
            dbg_n[0] += 1

        def cload(dst, src, **kw):
            S.dma("sp", "const", dst, src, writes=[r_const], **kw)
        S.op("pool", lambda e: e.memset(pin1[:], 0.0), writes=[r_const])
        S.op("pool", lambda e: e.memset(pin2[:], 0.0), writes=[r_const])
        cload(pin1[0:96, :], cw_d[0].rearrange("k (c p) -> (k c) p", p=128))
        cload(pin1[96:120, :], cb_d[0].rearrange("(c p) -> c p", p=128))
        cload(pin2[0:8, :], nmw_d[0].rearrange("(c p) -> c p", p=128))
        cload(pin2[8:16, :], nfw_d[0].rearrange("(c p) -> c p", p=128))
        cload(pin2[16:32, :], snw_d[0].rearrange("(c p) -> c p", p=128))
        cload(pin2[32:56, :], scw_d[0].rearrange("k (c p) -> (k c) p", p=128))
        cload(bc[:, 0, :], dtb_d.partition_broadcast(128))
        cload(bc[:, 1, :], alog_d.partition_broadcast(128))
        cload(bc[:, 2, :], dsk_d.partition_broadcast(128))
        cload(fnw[:], fnw_d.rearrange("(o d) -> o d", o=1).partition_broadcast(128))

        def mkmask(dst_bf, cmp_op, base):
            S.op("pool", lambda e: e.memset(mk_f[:], 1.0), writes=[r_mk])
            S.op("pool", lambda e: e.affine_select(out=mk_f[:], in_=mk_f[:], pattern=[[1, 128]],
                                                   compare_op=cmp_op, fill=0.0, base=base,
                                                   channel_multiplier=-1), reads=[r_mk], writes=[r_mk])
            S.op("pool", lambda e: e.tensor_copy(out=dst_bf[:], in_=mk_f[:]), reads=[r_mk], writes=[r_const])
        mkmask(tri_le, ALU.is_ge, 0)
        S.op("pool", lambda e: e.memset(mk_f[:], 1.0), writes=[r_mk])
        S.op("pool", lambda e: e.affine_select(out=mk_f[:], in_=mk_f[:], pattern=[[-1, 128]],
                                               compare_op=ALU.is_ge, fill=0.0, base=-1,
                                               channel_multiplier=1), reads=[r_mk], writes=[r_mk])
        S.op("pool", lambda e: e.tensor_copy(out=mstrict[:], in_=mk_f[:]), reads=[r_mk], writes=[r_const])
        S.op("pool", lambda e: e.memset(ones_b[:], 1.0), writes=[r_const])
        S.op("pool", lambda e: e.memset(ident_f[:], 1.0), writes=[r_const])
        S.op("pool", lambda e: e.affine_select(out=ident_f[:], in_=ident_f[:], pattern=[[1, 128]],
                                               compare_op=ALU.is_equal, fill=0.0, base=0,
                                               channel_multiplier=-1), reads=[r_const], writes=[r_const])
        S.op("pool", lambda e: e.tensor_copy(out=ident_b[:], in_=ident_f[:]), reads=[r_const], writes=[r_const])
        S.op("pool", lambda e: e.memset(halo[:], 0.0), writes=[r_halo])
        S.op("pool", lambda e: e.memset(halo2[:], 0.0), writes=[r_halo2])
        for g in range(4):
            S.op("pool", lambda e, g=g: e.memset(state[:, g, :], 0.0), writes=[r_state[g]])
            S.op("pool", lambda e, g=g: e.memset(stateb[:, g, :], 0.0), writes=[r_stateb[g]])
        bk, rb_ = getbank()
        S.op("pe", lambda e: e.transpose(out=bk[:, 0:120], in_=pin1[0:120, :], identity=ident_f[0:120, 0:120]),
             reads=[r_const], writes=[rb_])
        S.op("dve", lambda e: e.tensor_copy(out=pt1[:], in_=bk[:, 0:120]), reads=[rb_], writes=[r_const])
        bk, rb_ = getbank()
        S.op("pe", lambda e: e.transpose(out=bk[:, 0:56], in_=pin2[0:56, :], identity=ident_f[0:56, 0:56]),
             reads=[r_const], writes=[rb_])
        S.op("dve", lambda e: e.tensor_copy(out=pt2[:], in_=bk[:, 0:56]), reads=[rb_], writes=[r_const])
        S.op("act", lambda e: e.activation(out=bc[:, 3, :], in_=bc[:, 1, :], func=AF.Exp), reads=[r_const], writes=[r_const])
        S.op("dve", lambda e: e.tensor_scalar(out=bc[:, 1, :], in0=bc[:, 3, :], scalar1=-1.0, scalar2=None, op0=ALU.mult),
             reads=[r_const], writes=[r_const])
        w_in2 = win_d[0]
        ci = [0]

        def cast(dst, src, res):
            S.dma("pool", "cast%d" % ci[0], dst, src, writes=[res])
            ci[0] += 1
        r_win_parts = [Res("win_p%d" % i) for i in range(8)]
        cast(win_s[:, 10240:10272], w_in2[:, 5120:5152], r_win_parts[2])
        cast(win_s[:, 2048:5120], w_in2[:, 2048:5120], r_win_parts[1])
        cast(win_s[:, 0:2048], w_in2[:, 0:2048], r_win_parts[0])
        for t3 in range(3):
            dst = win_s[:, 5120:8192].rearrange("k (j t c) -> k j t c", j=8, t=3, c=128)[:, :, t3, :]
            src = w_in2[:, 5152 + t3 * 1024: 5152 + (t3 + 1) * 1024].rearrange("k (j c) -> k j c", c=128)
            cast(dst, src, r_win_parts[3 + t3])
        cast(win_s[:, 8192:10240], w_in2[:, 8224:10272], r_win_parts[6])
        r_win_all = r_win_parts[:7]
        late_casts = []

        def cast_chunks(dst, src, reslist, name, nch):
            rows = dst.shape[0]
            step = rows // nch
            for i in range(nch):
                r_ = Res("lc_%s%d" % (name, i))
                reslist.append(r_)
                late_casts.append(lambda i=i, r_=r_: S.dma("pool", "lc_" + name, dst[i * step:(i + 1) * step, :],
                                                           src[i * step:(i + 1) * step, :], writes=[r_]))
        r_wssd, r_wsc, r_wo, r_wg, r_wu, r_wd = [], [], [], [], [], []
        cast_chunks(wssd_s, wssd_d[0], r_wssd, "wssd", 4)
        cast_chunks(wsc_s, wsc_d[0], r_wsc, "wsc", 2)
        cast_chunks(wo_s, wo_d[0], r_wo, "wo", 2)
        cast_chunks(wg_s, wg_d[0], r_wg, "wg", 4)
        cast_chunks(wu_s, wu_d[0], r_wu, "wu", 4)
        cast_chunks(wd_s, wd_d[0], r_wd, "wd", 4)

        S.dma("sp", "const", wdt[:], win_s.rearrange("(kc p) n -> p kc n", p=128)[:, :, 10240:10272],
              reads=[r_win_parts[2]], writes=[r_const])
        RC = [r_const]
        for i in range(4):
            for src in (r_const, r_mk):
                for d_ in (src.reads, src.writes):
                    for k_, v_ in d_.items():
                        if r_tmpf[i].reads.get(k_, 0) < v_:
                            r_tmpf[i].reads[k_] = v_

        wsl_i = [0]

        def wload(src_view, kcn, ncols, src_res):
            i = wsl_i[0] % NWSLOT
            wsl_i[0] += 1
            v = wslots[i][:, 0:kcn * ncols].rearrange("p (k n) -> p k n", k=kcn)
            S.dma("sp", "w%d" % i, v, src_view, reads=src_res, writes=[r_wslot[i]])
            return v, r_wslot[i]

        win_v = win_s.rearrange("(kc p) n -> p kc n", p=128)

        def rmsnorm_to_T(sti, wcol0, tag):
            xs_ = xres[:, sti, :]
            S.op("act", lambda e: e.activation(out=hnb[:], in_=xs_, func=AF.Square, accum_out=stat[:, 0:1]),
                 reads=[r_xres[sti]], writes=[r_hnb, r_stat])
            S.op("act", lambda e: e.activation(out=stat[:, 1:2], in_=stat[:, 0:1], func=AF.Ln, scale=1.0 / D, bias=EPS),
                 reads=[r_stat], writes=[r_stat])
            S.op("act", lambda e: e.activation(out=stat[:, 2:3], in_=stat[:, 1:2], func=AF.Exp, scale=-0.5),
                 reads=[r_stat], writes=[r_stat])
            S.op("act", lambda e: e.activation(out=hnb[:], in_=xs_, func=AF.Copy, scale=stat[:, 2:3]),
                 reads=[r_xres[sti], r_stat], writes=[r_hnb])
            tp, rtp = gettp()
            for kc in range(8):
                S.op("pe", lambda e, kc=kc: e.transpose(out=tp[:, kc * 128:(kc + 1) * 128],
                                                        in_=hnb[:, kc * 128:(kc + 1) * 128], identity=ident_b[:]),
                     reads=[r_hnb] + RC, writes=[rtp])
            S.op("dve", lambda e: e.tensor_tensor(
                out=hnT[:, :, sti * 128:(sti + 1) * 128],
                in0=tp[:].rearrange("p (k t) -> p k t", k=8),
                in1=pt2[:, wcol0:wcol0 + 8].unsqueeze(2).to_broadcast([128, 8, 128]), op=ALU.mult),
                reads=[rtp] + RC, writes=r_hnT)

        def proj_fm(wv, rw, c0, kcn, rhs_of, rhs_res):
            bk, rbk = getbank()
            for kc in range(kcn):
                S.op("pe", lambda e, kc=kc: e.matmul(bk[:], lhsT=wv[:, kc, c0:c0 + 128], rhs=rhs_of(kc),
                                                     start=(kc == 0), stop=(kc == kcn - 1)),
                     reads=[rw, rhs_res[kc]], writes=[rbk])
            return bk, rbk

        for ti in range(NT):
            t0 = ti * TT
            for sti in range(NST):
                S.dma("sp", "x%d" % sti, xres[:, sti, :], x_d[t0 + sti * 128:t0 + (sti + 1) * 128, :],
                      writes=[r_xres[sti]])
            for sti in range(NST):
                rmsnorm_to_T(sti, 0, "mix")

            dbg('hnT', hnT[:], r_hnT, ti)
            p4w = {}

            def p4_s1(c):
                blk, j4 = divmod(c, 4)
                if j4 == 0:
                    p4w[blk] = wload(win_v[:, :, 2048 + blk * 512:2048 + (blk + 1) * 512], 8, 512, [r_win_parts[1]])
                wv, rw = p4w[blk]
                bk, rbk = proj_fm(wv, rw, j4 * 128, 8, lambda kc: hnT[:, kc, :], r_hnT)
                xb, rxb = gettmp()
                acc, racc = gettmp()
                S.op("act", lambda e: e.activation(out=xb[:, 3:515], in_=bk[:], func=AF.Copy),
                     reads=[rbk], writes=[rxb])
                S.op("act", lambda e: e.activation(out=xb[:, 0:3], in_=halo[:, c, :], func=AF.Copy),
                     reads=[r_halo], writes=[rxb])
                S.op("act", lambda e: e.activation(out=halo[:, c, :], in_=bk[:, 509:512], func=AF.Copy),
                     reads=[rbk], writes=[r_halo])
                S.op("act", lambda e: e.activation(out=acc[:, 0:512], in_=bk[:], func=AF.Identity,
                                                   scale=pt1[:, 72 + c:73 + c], bias=pt1[:, 96 + c:97 + c]),
                     reads=[rbk] + RC, writes=[racc])
                return (c, xb, rxb, acc, racc)

            def p4_s2(items):
                for k in range(3):
                    for (c, xb, rxb, acc, racc) in items:
                        S.op("dve", lambda e: e.scalar_tensor_tensor(
                            out=acc[:, 0:512], in0=xb[:, k:k + 512], scalar=pt1[:, k * 24 + c:k * 24 + c + 1],
                            in1=acc[:, 0:512], op0=ALU.mult, op1=ALU.add),
                            reads=[rxb, racc] + RC, writes=[racc])
                for (c, xb, rxb, acc, racc) in items:
                    if c < 20:
                        xo, rxo = xsT[c % 2], r_xsT[c % 2]
                        if c >= 16:
                            S.op("act", lambda e: e.activation(out=BT[:, c - 16, :], in_=acc[:, 0:512], func=AF.Silu),
                                 reads=[racc], writes=[r_BT[c - 16]])
                            src, rsrc = BT[:, c - 16, :], r_BT[c - 16]
                        else:
                            S.op("act", lambda e: e.activation(out=xo[:], in_=acc[:, 0:512], func=AF.Silu),
                                 reads=[racc], writes=[rxo])
                            src, rsrc = xo[:], rxo
                        tp, rtp = gettp()
                        for sti in range(NST):
                            S.op("pe", lambda e, sti=sti: e.transpose(out=tp[:, sti * 128:(sti + 1) * 128],
                                                                      in_=src[:, sti * 128:(sti + 1) * 128],
                                                                      identity=ident_b[:]),
                                 reads=[rsrc] + RC, writes=[rtp])
                        if c < 16:
                            S.op("dve", lambda e: e.tensor_copy(out=xs_all[:, :, c * 128:(c + 1) * 128],
                                                                in_=tp[:, 0:512].rearrange("p (s t) -> p s t", s=NST)),
                                 reads=[rtp], writes=r_xs)
                        else:
                            S.op("dve", lambda e: e.tensor_copy(out=b_all[:, :, (c - 16) * 128:(c - 15) * 128],
                                                                in_=tp[:, 0:512].rearrange("p (s t) -> p s t", s=NST)),
                                 reads=[rtp], writes=r_ball)
                    else:
                        S.op("act", lambda e: e.activation(out=CT[:, c - 20, :], in_=acc[:, 0:512], func=AF.Silu),
                             reads=[racc], writes=[r_CT[c - 20]])

            pend = [p4_s1(0), p4_s1(1)]
            for pp in range(12):
                nxt = [p4_s1(2 * pp + 2), p4_s1(2 * pp + 3)] if pp < 11 else []
                p4_s2(pend)
                pend = nxt

            def z_unit(sti, q):
                def f():
                    wv, rw = wload(win_v[:, :, q * 512:(q + 1) * 512], 8, 512, [r_win_parts[0]])
                    bk, rbk = getbank()
                    for kc in range(8):
                        S.op("pe", lambda e, kc=kc: e.matmul(bk[:], lhsT=hnT[:, kc, sti * 128:(sti + 1) * 128],
                                                             rhs=wv[:, kc, :], start=(kc == 0), stop=(kc == 7)),
                             reads=[rw, r_hnT[kc]], writes=[rbk])
                    th, rth = gettmp()
                    S.op("act", lambda e: e.activation(out=th[:, 0:512], in_=bk[:], func=AF.Tanh, scale=0.5),
                         reads=[rbk], writes=[rth])
                    S.op("dve", lambda e: e.scalar_tensor_tensor(out=zs[sti % 2][:, q * 512:(q + 1) * 512], in0=th[:, 0:512],
                                                                 scalar=1.0, in1=bk[:], op0=ALU.add, op1=ALU.mult),
                         reads=[rth, rbk], writes=[r_zs[sti % 2]])
                return f

            def sconv_unit(j):
                def f():
                    wv, rw = wload(win_v[:, :, 5120 + j * 384:5120 + (j + 1) * 384], 8, 384, r_win_parts[3:6])
                    pb, rpb = proj_fm(wv, rw, 0, 8, lambda kc: hnT[:, kc, :], r_hnT)
                    pc, rpc = proj_fm(wv, rw, 128, 8, lambda kc: hnT[:, kc, :], r_hnT)
                    px, rpx = proj_fm(wv, rw, 256, 8, lambda kc: hnT[:, kc, :], r_hnT)
                    sc_, rsc = gettmp()
                    S.op("act", lambda e: e.activation(out=sc_[:, 0:512], in_=pc[:], func=AF.Copy), reads=[rpc], writes=[rsc])
                    prd, rprd = gettmp()
                    S.op("dve", lambda e: e.tensor_tensor(out=prd[:, 2:514], in0=px[:], in1=sc_[:, 0:512], op=ALU.mult),
                         reads=[rpx, rsc], writes=[rprd])
                    S.op("pool", lambda e: e.tensor_copy(out=prd[:, 0:2], in_=halo2[:, j, :]), reads=[r_halo2], writes=[rprd])
                    S.op("pool", lambda e: e.tensor_copy(out=halo2[:, j, :], in_=prd[:, 512:514]), reads=[rprd], writes=[r_halo2])
                    acc, racc = gettmp()
                    S.op("dve", lambda e: e.tensor_scalar(out=acc[:, 0:512], in0=prd[:, 0:512], scalar1=pt2[:, 32 + j:33 + j],
                                                          scalar2=None, op0=ALU.mult),
                         reads=[rprd] + RC, writes=[racc])
                    for k in range(1, 3):
                        S.op("dve", lambda e, k=k: e.scalar_tensor_tensor(
                            out=acc[:, 0:512], in0=prd[:, k:k + 512], scalar=pt2[:, 32 + k * 8 + j:33 + k * 8 + j],
                            in1=acc[:, 0:512], op0=ALU.mult, op1=ALU.add),
                            reads=[rprd, racc] + RC, writes=[racc])
                    S.op("dve", lambda e: e.tensor_tensor(out=uT[:, j, :], in0=acc[:, 0:512], in1=pb[:], op=ALU.mult),
                         reads=[racc, rpb], writes=[r_uT[j]])
                return f

            def gate_unit(blk):
                def f():
                    wv, rw = wload(win_v[:, :, 8192 + blk * 512:8192 + (blk + 1) * 512], 8, 512, [r_win_parts[6]])
                    for j4 in range(4):
                        gi = blk * 4 + j4
                        bk, rbk = proj_fm(wv, rw, j4 * 128, 8, lambda kc: hnT[:, kc, :], r_hnT)
                        S.op("act", lambda e: e.activation(out=gsig[:, gi, :], in_=bk[:], func=AF.Tanh, scale=0.5),
                             reads=[rbk], writes=[r_gsig[gi]])
                return f

            def V(i):
                return sm[:, i * 128:(i + 1) * 128]

            def V3(i):
                return V(i).rearrange("p (s h) -> p s h", s=NST)

            xsd_buf = [xsd, xdt]
            r_xsd_buf = [r_xsd, r_xdt]

            def emit_xsd(st2):
                S.op("pool", lambda e: e.tensor_tensor(
                    out=xsd_buf[st2 % 2][:].rearrange("p (h d) -> p h d", h=32),
                    in0=xs_all[:, st2, :].rearrange("p (h d) -> p h d", h=32),
                    in1=bc[:, 2, :].unsqueeze(2).to_broadcast([128, 32, 64]), op=ALU.mult),
                    reads=[r_xs[st2]] + RC, writes=[r_xsd_buf[st2 % 2]])

            def ssd_gen():
                bk, rbk = getbank()
                for sti in range(NST):
                    for kc in range(8):
                        S.op("pe", lambda e, kc=kc, sti=sti: e.matmul(bk[:, sti * 32:(sti + 1) * 32],
                                                                      lhsT=hnT[:, kc, sti * 128:(sti + 1) * 128],
                                                                      rhs=wdt[:, kc, :], start=(kc == 0), stop=(kc == 7)),
                             reads=[r_hnT[kc]] + RC, writes=[rbk])

                def b4(v):
                    return v.unsqueeze(1).to_broadcast([128, NST, 32])
                S.op("dve", lambda e: e.tensor_tensor(out=V3(0), in0=bk[:, 0:128].rearrange("p (s h) -> p s h", s=NST),
                                                      in1=b4(bc[:, 0, :]), op=ALU.add),
                     reads=[rbk] + RC, writes=[r_sm])
                S.op("act", lambda e: e.activation(out=V(0), in_=V(0), func=AF.Exp), reads=[r_sm], writes=[r_sm])
                S.op("act", lambda e: e.activation(out=V(1), in_=V(0), func=AF.Ln, bias=1.0), reads=[r_sm], writes=[r_sm])
                S.op("act", lambda e: e.activation(out=V(11), in_=V(1), func=AF.Ln), reads=[r_sm], writes=[r_sm])
                S.op("dve", lambda e: e.tensor_tensor(out=V3(2), in0=V3(1), in1=b4(bc[:, 1, :]), op=ALU.mult),
                     reads=[r_sm] + RC, writes=[r_sm])
                dh0 = dhl[:, 0, :, :].rearrange("p s h -> p (s h)")
                dh1 = dhl[:, 1, :, :].rearrange("p s h -> p (s h)")
                S.op("dve", lambda e: e.tensor_copy(out=dh0, in_=V(2)), reads=[r_sm], writes=[r_dhl])
                S.op("dve", lambda e: e.tensor_tensor(out=V(3), in0=V(2), in1=dh0, op=ALU.subtract),
                     reads=[r_sm, r_dhl], writes=[r_sm])
                S.op("dve", lambda e: e.tensor_copy(out=dh1, in_=V(3)), reads=[r_sm], writes=[r_dhl])
                bk2, rbk2 = getbank()
                for sti in range(NST):
                    for (c0, lhs) in ((0, tri_le), (32, ones_b)):
                        for hl in range(2):
                            S.op("pe", lambda e, c0=c0, lhs=lhs, hl=hl, sti=sti: e.matmul(
                                bk2[:, sti * 64 + c0:sti * 64 + c0 + 32], lhsT=lhs[:], rhs=dhl[:, hl, sti, :],
                                start=(hl == 0), stop=(hl == 1)),
                                reads=[r_dhl] + RC, writes=[rbk2])
                S.op("dve", lambda e: e.tensor_copy(out=sm[:, 512:768], in_=bk2[:, 0:256]), reads=[rbk2], writes=[r_sm])
                at3 = sm[:, 512:768].rearrange("p (s c) -> p s c", s=NST)
                acs3, tots3 = at3[:, :, 0:32], at3[:, :, 32:64]
                S.op("dve", lambda e: e.tensor_tensor(out=V3(6), in0=tots3, in1=acs3, op=ALU.subtract), reads=[r_sm], writes=[r_sm])
                S.op("act", lambda e: e.activation(out=V3(7), in_=acs3, func=AF.Exp), reads=[r_sm], writes=[r_sm])
                S.op("act", lambda e: e.activation(out=V3(8), in_=tots3, func=AF.Exp), reads=[r_sm], writes=[r_sm])
                S.op("act", lambda e: e.activation(out=V(9), in_=V(6), func=AF.Exp), reads=[r_sm], writes=[r_sm])
                S.op("dve", lambda e: e.tensor_tensor(out=V(10), in0=V(1), in1=V(9), op=ALU.mult), reads=[r_sm], writes=[r_sm])
                yield ("pt", -1)

                for sti in range(NST):
                    tsl = slice(sti * 128, (sti + 1) * 128)
                    dt_s, eac, etot, w2 = V3(1)[:, sti, :], V3(7)[:, sti, :], V3(8)[:, sti, :], V3(10)[:, sti, :]
                    zcur, rz = zs[sti % 2], r_zs[sti % 2]
                    xs3 = xs_all[:, sti, :].rearrange("p (h d) -> p h d", h=32)

                    def bch(v):
                        return v.unsqueeze(2).to_broadcast([128, 32, 64])
                    xsd_c, r_xsd_c = xsd_buf[sti % 2], r_xsd_buf[sti % 2]
                    lndt_s = V3(11)[:, sti, :]
                    bkc, rbkc = getbank()
                    for g in range(4):
                        S.op("pe", lambda e, g=g: e.matmul(bkc[:, g * 128:(g + 1) * 128], lhsT=BT[:, g, tsl], rhs=CT[:, g, tsl],
                                                           start=True, stop=True),
                             reads=[r_BT[g], r_CT[g]], writes=[rbkc])
                    S.op("dve", lambda e: e.tensor_tensor(out=cbm[:], in0=bkc[:].rearrange("p (g l) -> p g l", g=4),
                                                          in1=tri_le[:].unsqueeze(1).to_broadcast([128, 4, 128]), op=ALU.mult),
                         reads=[rbkc] + RC, writes=[r_cbm])
                    yield ("pt", sti)

                    def stage_a(g):
                        R_, rR = Rg[g % 2], r_Rg[g % 2]
                        E_, rE = Eg[g % 2], r_Eg[g % 2]
                        S.op("pool", lambda e: e.tensor_tensor(
                            out=R_[:], in0=tri_le[:].unsqueeze(1).to_broadcast([128, 8, 128]),
                            in1=dhl[:, 0, sti, g * 8:(g + 1) * 8].unsqueeze(2).to_broadcast([128, 8, 128]), op=ALU.mult),
                            reads=[r_dhl] + RC, writes=[rR])
                        for half in range(2):
                            bs, rbs = getbank()
                            S.op("pe", lambda e: e.matmul(bs[:], lhsT=mstrict[:], rhs=R_[:, half * 4:(half + 1) * 4, :],
                                                          start=True, stop=True),
                                 reads=[rR] + RC, writes=[rbs])
                            for r4 in range(4):
                                hh = g * 8 + half * 4 + r4
                                S.op("act", lambda e, r4=r4, hh=hh: e.activation(out=E_[:, half * 4 + r4, :],
                                                                                 in_=bs[:, r4 * 128:(r4 + 1) * 128], func=AF.Exp,
                                                                                 bias=lndt_s[:, hh:hh + 1]),
                                     reads=[rbs, r_sm], writes=[rE])
                        S.op("dve", lambda e: e.tensor_tensor(out=scr_[g % 2][:], in0=E_[:],
                                                              in1=cbm[:, g, :].unsqueeze(1).to_broadcast([128, 8, 128]), op=ALU.mult),
                             reads=[rE, r_cbm], writes=[r_scr[g % 2]])

                    def stage_b(g):
                        sc, rsc = scr_[g % 2], r_scr[g % 2]
                        yd, ryd = getbank()
                        S.op("pe", lambda e: e.matmul(yd[:], lhsT=ident_b[:], rhs=xsd_c[:, g * 512:(g + 1) * 512], start=True, stop=False),
                             reads=[r_xsd_c] + RC, writes=[ryd])
                        for r in range(8):
                            S.op("pe", lambda e, r=r: e.matmul(yd[:, r * 64:(r + 1) * 64], lhsT=sc[:, r, :],
                                                               rhs=xs_all[:, sti, (g * 8 + r) * 64:(g * 8 + r + 1) * 64],
                                                               start=False, stop=(r == 7)),
                                 reads=[rsc, r_xs[sti]], writes=[ryd])
                        yo, ryo = getbank()
                        S.op("pe", lambda e: e.matmul(yo[:], lhsT=CT[:, g, tsl], rhs=stateb[:, g, :], start=True, stop=True),
                             reads=[r_CT[g], r_stateb[g]], writes=[ryo])
                        yg = yg4[:, g * 512:(g + 1) * 512]
                        S.op("dve", lambda e: e.tensor_tensor(out=yg.rearrange("p (r d) -> p r d", r=8),
                                                              in0=yo[:].rearrange("p (r d) -> p r d", r=8),
                                                              in1=eac[:, g * 8:(g + 1) * 8].unsqueeze(2).to_broadcast([128, 8, 64]),
                                                              op=ALU.mult),
                             reads=[ryo, r_sm], writes=[r_yg4[g]])
                        S.op("dve", lambda e: e.tensor_tensor(out=yg, in0=yg, in1=yd[:], op=ALU.add),
                             reads=[ryd, r_yg4[g]], writes=[r_yg4[g]])
                        S.op("dve", lambda e: e.tensor_tensor(out=yg, in0=yg, in1=zcur[:, g * 512:(g + 1) * 512], op=ALU.mult),
                             reads=[r_yg4[g], rz], writes=[r_yg4[g]])
                        jk, rjk = gettmp()
                        S.op("act", lambda e: e.activation(out=jk[:, 0:512], in_=yg, func=AF.Square,
                                                           accum_out=stat[:, 16 + g:17 + g]),
                             reads=[r_yg4[g]], writes=[rjk, r_stat])

                    stage_a(0)
                    yield ("pt", sti)
                    for g in range(4):
                        if g + 1 < 4:
                            stage_a(g + 1)
                            yield ("pt", sti)
                        if g == 2:
                            S.op("pool", lambda e: e.tensor_tensor(out=xdtw[:].rearrange("p (h d) -> p h d", h=32), in0=xs3, in1=bch(w2), op=ALU.mult),
                                 reads=[r_xs[sti], r_sm], writes=[r_xdtw])
                            if sti + 1 < NST:
                                emit_xsd(sti + 1)
                        stage_b(g)
                        yield ("pt", sti)
                    S.op("act", lambda e: e.activation(out=stat[:, 20:24], in_=stat[:, 16:20], func=AF.Ln, scale=1.0 / 512, bias=4.0 * EPS),
                         reads=[r_stat], writes=[r_stat])
                    S.op("act", lambda e: e.activation(out=stat[:, 24:28], in_=stat[:, 20:24], func=AF.Exp, scale=-0.5),
                         reads=[r_stat], writes=[r_stat])
                    for g in range(4):
                        yb, ryb = ygb[g % 2], r_ygb[g % 2]
                        S.op("act", lambda e: e.activation(out=yb[:], in_=yg4[:, g * 512:(g + 1) * 512], func=AF.Copy,
                                                           scale=stat[:, 24 + g:25 + g]),
                             reads=[r_yg4[g], r_stat], writes=[ryb])
                        tp, rtp = gettp()
                        for cc in range(4):
                            S.op("pe", lambda e, cc=cc: e.transpose(out=tp[:, cc * 128:(cc + 1) * 128],
                                                                    in_=yb[:, cc * 128:(cc + 1) * 128], identity=ident_b[:]),
                                 reads=[ryb] + RC, writes=[rtp])
                        S.op("dve", lambda e: e.tensor_tensor(
                            out=yT[:, g * 4:(g + 1) * 4, tsl], in0=tp[:, 0:512].rearrange("p (c t) -> p c t", c=4),
                            in1=pt2[:, 16 + g * 4:16 + (g + 1) * 4].unsqueeze(2).to_broadcast([128, 4, 128]), op=ALU.mult),
                            reads=[rtp] + RC, writes=r_yT[g * 4:(g + 1) * 4])
                        yield ("pt", sti)
                    for g in range(4):
                        sn, rsn = getbank()
                        S.op("pe", lambda e: e.matmul(sn[:], lhsT=b_all[:, sti, g * 128:(g + 1) * 128], rhs=xdtw[:, g * 512:(g + 1) * 512],
                                                      start=True, stop=True),
                             reads=[r_ball[sti], r_xdtw], writes=[rsn])
                        S.op("dve", lambda e: e.tensor_tensor(out=state[:, g, :].rearrange("p (r d) -> p r d", r=8),
                                                              in0=state[:, g, :].rearrange("p (r d) -> p r d", r=8),
                                                              in1=etot[:, g * 8:(g + 1) * 8].unsqueeze(2).to_broadcast([128, 8, 64]),
                                                              op=ALU.mult),
                             reads=[r_state[g], r_sm], writes=[r_state[g]])
                        S.op("dve", lambda e: e.tensor_tensor(out=state[:, g, :], in0=state[:, g, :], in1=sn[:], op=ALU.add),
                             reads=[r_state[g], rsn], writes=[r_state[g]])
                        S.op("pool", lambda e: e.tensor_copy(out=stateb[:, g, :], in_=state[:, g, :]),
                             reads=[r_state[g]], writes=[r_stateb[g]])
                        yield ("pt", sti)
                    yield ("end", sti)

            others = [sconv_unit(j) for j in range(8)] + [gate_unit(b) for b in range(4)]
            fill = []
            for sti in range(NST):
                f_ = [z_unit(sti + 1, q) for q in range(4)] if sti + 1 < NST else []
                fill.append(f_ + others[sti * 3:(sti + 1) * 3])
            emit_xsd(0)
            for q in range(4):
                z_unit(0, q)()
            npt = 0
            for kind, sti in ssd_gen():
                if sti < 0:
                    continue
                if kind == "pt":
                    npt += 1
                    if late_casts:
                        late_casts.pop(0)()
                    if npt % 2 == 0 and fill[sti]:
                        fill[sti].pop(0)()
                else:
                    while fill[sti]:
                        fill[sti].pop(0)()
            while late_casts:
                late_casts.pop(0)()
            dbg('yT', yT[:], r_yT, ti)
            dbg('state', state[:], r_state, ti)
            dbg('uT', uT, r_uT, ti)
            dbg('gsig', gsig, r_gsig, ti)

            wssd_v = wssd_s.rearrange("(kc p) n -> p kc n", p=128)
            wsc_v = wsc_s.rearrange("(kc p) n -> p kc n", p=128)
            for ob in range(2):
                wa0, rwa0 = wload(wssd_v[:, 0:8, ob * 512:(ob + 1) * 512], 8, 512, r_wssd)
                wa1, rwa1 = wload(wssd_v[:, 8:16, ob * 512:(ob + 1) * 512], 8, 512, r_wssd)
                wb, rwb = wload(wsc_v[:, :, ob * 512:(ob + 1) * 512], 8, 512, r_wsc)
                for j4 in range(4):
                    oc = ob * 4 + j4
                    pa, rpa = getbank()
                    for kc in range(16):
                        wv_, rw_ = (wa0, rwa0) if kc < 8 else (wa1, rwa1)
                        S.op("pe", lambda e, kc=kc, wv_=wv_: e.matmul(pa[:], lhsT=wv_[:, kc % 8, j4 * 128:(j4 + 1) * 128],
                                                                      rhs=yT[:, kc, :], start=(kc == 0), stop=(kc == 15)),
                             reads=[rw_, r_yT[kc]], writes=[rpa])
                    pb, rpb = proj_fm(wb, rwb, j4 * 128, 8, lambda kc: uT[:, kc, :], r_uT)
                    m1, rm1 = gettmp()
                    m2, rm2 = gettmp()
                    S.op("dve", lambda e: e.scalar_tensor_tensor(out=m1[:, 0:512], in0=gsig[:, oc, :], scalar=1.0, in1=pa[:],
                                                                 op0=ALU.add, op1=ALU.mult),
                         reads=[rpa, r_gsig[oc]], writes=[rm1])
                    S.op("dve", lambda e: e.scalar_tensor_tensor(out=m2[:, 0:512], in0=gsig[:, 8 + oc, :], scalar=1.0, in1=pb[:],
                                                                 op0=ALU.add, op1=ALU.mult),
                         reads=[rpb, r_gsig[8 + oc]], writes=[rm2])
                    S.op("dve", lambda e: e.tensor_tensor(out=hnT[:, oc, :], in0=m1[:, 0:512], in1=m2[:, 0:512], op=ALU.add),
                         reads=[rm1, rm2], writes=[r_hnT[oc]])

            dbg('mT', hnT[:], r_hnT, ti)
            wo_v = wo_s.rearrange("(kc p) n -> p kc n", p=128)
            for ob in range(2):
                wv, rw = wload(wo_v[:, :, ob * 512:(ob + 1) * 512], 8, 512, r_wo)
                for sti in range(NST):
                    bk, rbk = getbank()
                    for kc in range(8):
                        S.op("pe", lambda e, kc=kc: e.matmul(bk[:], lhsT=hnT[:, kc, sti * 128:(sti + 1) * 128], rhs=wv[:, kc, :],
                                                             start=(kc == 0), stop=(kc == 7)),
                             reads=[rw, r_hnT[kc]], writes=[rbk])
                    S.op("dve", lambda e: e.scalar_tensor_tensor(out=xres[:, sti, ob * 512:(ob + 1) * 512], in0=bk[:], scalar=0.5,
                                                                 in1=xres[:, sti, ob * 512:(ob + 1) * 512],
                                                                 op0=ALU.mult, op1=ALU.add),
                         reads=[rbk, r_xres[sti]], writes=[r_xres[sti]])

            dbg('h1', xres[:], r_xres, ti)
            for sti in range(NST):
                rmsnorm_to_T(sti, 8, "ffn")

            wg_v = wg_s.rearrange("(kc p) n -> p kc n", p=128)
            wu_v = wu_s.rearrange("(kc p) n -> p kc n", p=128)
            for blk in range(6):
                ncol = 512 if blk < 5 else 256
                wgv, rwg = wload(wg_v[:, :, blk * 512:blk * 512 + ncol], 8, ncol, r_wg)
                wuv, rwu = wload(wu_v[:, :, blk * 512:blk * 512 + ncol], 8, ncol, r_wu)
                for j4 in range(ncol // 128):
                    f = blk * 4 + j4
                    pg, rpg = proj_fm(wgv, rwg, j4 * 128, 8, lambda kc: hnT[:, kc, :], r_hnT)
                    pu, rpu = proj_fm(wuv, rwu, j4 * 128, 8, lambda kc: hnT[:, kc, :], r_hnT)
                    sg, rsg = gettmp()
                    S.op("act", lambda e: e.activation(out=sg[:, 0:512], in_=pg[:], func=AF.Silu), reads=[rpg], writes=[rsg])
                    S.op("dve", lambda e: e.tensor_tensor(out=actT[:, f, :], in0=sg[:, 0:512], in1=pu[:], op=ALU.mult),
                         reads=[rsg, rpu], writes=[r_actT[f]])

            dbg('actT', actT, r_actT, ti)
            wd_v = wd_s.rearrange("(kc p) n -> p kc n", p=128)
            for ob in range(2):
                accs = [getbank() for _ in range(NST)]
                for (k0, k1) in ((0, 8), (8, 16), (16, 22)):
                    wv_, rw_ = wload(wd_v[:, k0:k1, ob * 512:(ob + 1) * 512], k1 - k0, 512, r_wd)
                    for sti in range(NST):
                        bk, rbk = accs[sti]
                        for kc in range(k0, k1):
                            S.op("pe", lambda e, kc=kc: e.matmul(bk[:], lhsT=actT[:, kc, sti * 128:(sti + 1) * 128],
                                                                 rhs=wv_[:, kc - k0, :], start=(kc == 0), stop=(kc == 21)),
                                 reads=[rw_, r_actT[kc]], writes=[rbk])
                for sti in range(NST):
                    bk, rbk = accs[sti]
                    S.op("dve", lambda e: e.tensor_tensor(out=xres[:, sti, ob * 512:(ob + 1) * 512],
                                                          in0=xres[:, sti, ob * 512:(ob + 1) * 512], in1=bk[:], op=ALU.add),
                         reads=[rbk, r_xres[sti]], writes=[r_xres[sti]])
            dbg('h2', xres[:], r_xres, ti)

            for sti in range(NST):
                S.op("act", lambda e: e.activation(out=hnb[:], in_=xres[:, sti, :], func=AF.Square, accum_out=stat[:, 8:9]),
                     reads=[r_xres[sti]], writes=[r_hnb, r_stat])
                S.op("act", lambda e: e.activation(out=stat[:, 9:10], in_=stat[:, 8:9], func=AF.Ln, scale=1.0 / D, bias=EPS),
                     reads=[r_stat], writes=[r_stat])
                S.op("act", lambda e: e.activation(out=stat[:, 10:11], in_=stat[:, 9:10], func=AF.Exp, scale=-0.5),
                     reads=[r_stat], writes=[r_stat])
                ob = yg4[:, (sti % 2) * 1024:(sti % 2 + 1) * 1024]
                rob = r_yg4[(sti % 2) * 2:(sti % 2) * 2 + 2]
                S.op("dve", lambda e: e.scalar_tensor_tensor(out=ob, in0=xres[:, sti, :], scalar=stat[:, 10:11],
                                                             in1=fnw[:], op0=ALU.mult, op1=ALU.mult),
                     reads=[r_xres[sti], r_stat] + RC, writes=rob)
                S.dma("pool", "out", out_d[t0 + sti * 128:t0 + (sti + 1) * 128, :], ob, reads=rob)

        S.wait_all("pool", r_yg4)
        S.wait_all("sp", r_yg4)
    return nc, S


_NAMES = ["x", "norm_mix_w", "w_in", "ssd_conv_w", "ssd_conv_b", "dt_bias", "a_log", "d_skip", "ssd_norm_w",
          "w_ssd_proj", "sconv_w", "w_sconv_proj", "w_o", "norm_ffn_w", "w_gate", "w_up", "w_down", "final_norm_w"]


def kernel(**inputs):
    x = np.asarray(inputs["x"], dtype=np.float32)
    B, T, _ = x.shape
    nc, _ = build(T)
    shared = {k: np.ascontiguousarray(np.asarray(inputs[k], dtype=np.float32)) for k in _NAMES if k != "x"}
    in_maps = []
    for b in range(B):
        m = dict(shared)
        m["x"] = np.ascontiguousarray(x[b])
        in_maps.append(m)
    res = run_bass_kernel_spmd(nc, in_maps, core_ids=list(range(B)))
    out = np.stack([np.asarray(r["out"]) for r in res.results], axis=0)
    return out.astype(np.float32)
```
